# Optimizing a Trainium2 kernel written in Bass

```python
import math
import jax, jax.numpy as jnp
from jax import lax
import numpy as np

D_MODEL = 2048
BATCH = 4
SEQ = 2048
DEPTH = 4
DEC_BATCH = 8
DEC_SEQ = 4
PAST_LEN = 16384
PAGE_SIZE = 128

HD = 64
D_ATT = D_MODEL // 2
H_A = D_ATT // HD
DILATED_PATTERNS = ((128, 1), (512, 4), (2048, 16))
MAX_WINDOW = 2048
ROPE_THETA = 10000.0
D_SSD = D_MODEL // 2
SSD_HEAD_DIM = 64
H_S = D_SSD // SSD_HEAD_DIM
SSD_STATE = 128
SSD_GROUPS = 4
SSD_CONV = 4
SSD_CHUNK = 128
D_XBC = D_SSD + 2 * SSD_GROUPS * SSD_STATE
D_MIX = D_ATT + D_SSD
D_IN_PROJ = 3 * D_ATT + D_SSD + D_XBC + H_S
IN_SPLITS = (D_ATT, 2 * D_ATT, 3 * D_ATT, 3 * D_ATT + D_SSD, 3 * D_ATT + D_SSD + D_XBC)
D_FF = 11 * D_MODEL // 4
FFN_CONV = 3
EPS = 1e-6

kernel_name = 'hybrid_dilated_attn_ssd_convffn_step'


def rms_norm(x, g):
    xf = x.astype(jnp.float32)
    y = xf * lax.rsqrt(jnp.mean(xf * xf, axis=-1, keepdims=True) + EPS)
    return (y * g.astype(jnp.float32)).astype(x.dtype)


def rope(x, pos):
    half = HD // 2
    inv = ROPE_THETA ** (-jnp.arange(half, dtype=jnp.float32) / half)
    ang = pos.astype(jnp.float32)[:, None] * inv[None, :]
    cos = jnp.cos(ang)[None, :, None, :]
    sin = jnp.sin(ang)[None, :, None, :]
    x1, x2 = x[..., :half], x[..., half:]
    return jnp.concatenate([x1 * cos - x2 * sin, x2 * cos + x1 * sin], axis=-1)


def causal_dwconv(x, buf, w, b):
    width = w.shape[0]
    L = x.shape[1]
    xp = jnp.concatenate([buf.astype(x.dtype), x], axis=1)
    y = b
    for t in range(width):
        y = y + xp[:, t:t + L] * w[t]
    return y, xp[:, xp.shape[1] - (width - 1):]


def dilated_attn_prompt(q, k, v, window, dil):
    bsz, S, H, hd = q.shape
    M = S // dil
    n = window // dil
    nb = -(-M // n)
    pad = nb * n - M

    def sub(a, front):
        a = a.reshape(bsz, M, dil, H, hd)
        a = jnp.pad(a, ((0, 0), (front, pad), (0, 0), (0, 0), (0, 0)))
        return a.reshape(bsz, -1, n, dil, H, hd)

    qb = sub(q, 0)
    kb = sub(k, n)
    vb = sub(v, n)
    kk = jnp.concatenate([kb[:, :-1], kb[:, 1:]], axis=2)
    vv = jnp.concatenate([vb[:, :-1], vb[:, 1:]], axis=2)
    s = jnp.einsum('bcqrhd,bckrhd->bcrhqk', qb, kk) * (hd ** -0.5)
    qi = jnp.arange(n)[:, None]
    ki = jnp.arange(2 * n)[None, :]
    dist = n + qi - ki
    key_pos = (jnp.arange(nb)[:, None, None] - 1) * n + ki
    valid = (dist >= 0) & (dist <= n) & (key_pos >= 0)
    s = jnp.where(valid[None, :, None, None], s, -jnp.inf)
    m = jnp.max(s, axis=-1, keepdims=True)
    p = jnp.exp(s - m)
    den = jnp.sum(p, axis=-1, keepdims=True)
    o = jnp.einsum('bcrhqk,bckrhd->bcqrhd', p / den, vv)
    lse = (m + jnp.log(den))[..., 0]
    o = o.reshape(bsz, nb * n, dil, H, hd)[:, :M].reshape(bsz, S, H, hd)
    lse = jnp.moveaxis(lse, -1, 2).reshape(bsz, nb * n, dil, H)[:, :M].reshape(bsz, S, H)
    return o, lse


def dilated_attn_step(q, k_all, v_all, window, dil):
    L = q.shape[1]
    hd = q.shape[-1]
    Lb = k_all.shape[1] - L
    n = window // dil
    idx = Lb + jnp.arange(L)[:, None] - dil * jnp.arange(n + 1)[None, :]
    valid = idx >= 0
    idx = jnp.maximum(idx, 0)
    kg = k_all[:, idx]
    vg = v_all[:, idx]
    s = jnp.einsum('blhd,blihd->blhi', q, kg) * (hd ** -0.5)
    s = jnp.where(valid[None, :, None, :], s, -jnp.inf)
    m = jnp.max(s, axis=-1, keepdims=True)
    p = jnp.exp(s - m)
    den = jnp.sum(p, axis=-1, keepdims=True)
    o = jnp.einsum('blhi,blihd->blhd', p / den, vg)
    return o, (m + jnp.log(den))[..., 0]


def attn_mixer(q, k, v, pos, q_norm_g, k_norm_g, win_k, win_v):
    bsz, L, _ = q.shape
    f32 = jnp.float32
    q = rope(rms_norm(q.reshape(bsz, L, H_A, HD).astype(f32), q_norm_g), pos)
    k = rope(rms_norm(k.reshape(bsz, L, H_A, HD).astype(f32), k_norm_g), pos)
    v = v.reshape(bsz, L, H_A, HD).astype(f32)
    if win_k is None:
        res = [dilated_attn_prompt(q, k, v, w, d) for (w, d) in DILATED_PATTERNS]
    else:
        k_all = jnp.concatenate([win_k.astype(f32), k], axis=1)
        v_all = jnp.concatenate([win_v.astype(f32), v], axis=1)
        res = [dilated_attn_step(q, k_all, v_all, w, d) for (w, d) in DILATED_PATTERNS]
    outs = jnp.stack([r[0] for r in res])
    lses = jnp.stack([r[1] for r in res])
    wts = jax.nn.softmax(lses, axis=0)
    out = jnp.sum(outs * wts[..., None], axis=0)
    keep = min(MAX_WINDOW, L)
    return out.reshape(bsz, L, D_ATT), k[:, L - keep:], v[:, L - keep:]


def ssd_chunked(x, a, B, C, h0, chunk):
    b, l, h, p = x.shape
    n = B.shape[-1]
    c = l // chunk
    x = x.reshape(b, c, chunk, h, p)
    a = a.reshape(b, c, chunk, h)
    B = B.reshape(b, c, chunk, h, n)
    C = C.reshape(b, c, chunk, h, n)
    a_cum = jnp.cumsum(a, axis=2)
    seg = a_cum[:, :, :, None, :] - a_cum[:, :, None, :, :]
    causal = jnp.tril(jnp.ones((chunk, chunk), dtype=bool))[None, None, :, :, None]
    decay = jnp.exp(jnp.where(causal, seg, -jnp.inf))
    cb = jnp.einsum('bcihn,bcjhn->bcijh', C, B)
    y_diag = jnp.einsum('bcijh,bcjhp->bcihp', cb * decay, x)
    decay_end = jnp.exp(a_cum[:, :, -1:, :] - a_cum)
    states = jnp.einsum('bcjhn,bcjh,bcjhp->bchpn', B, decay_end, x)
    chunk_decay = jnp.exp(a_cum[:, :, -1, :])

    def step(h_prev, inp):
        dec, st = inp
        return dec[:, :, None, None] * h_prev + st, h_prev

    h_last, h_in = lax.scan(step, h0, (jnp.moveaxis(chunk_decay, 1, 0), jnp.moveaxis(states, 1, 0)))
    h_in = jnp.moveaxis(h_in, 0, 1)
    y_off = jnp.einsum('bcihn,bchpn->bcihp', C, h_in) * jnp.exp(a_cum)[..., None]
    return (y_diag + y_off).reshape(b, l, h, p), h_last


def ssd_mixer(z, xbc, dt_raw, conv_buf, h0, conv_w, conv_b, dt_bias, a_log, d_skip, norm_g):
    f32 = jnp.float32
    bsz, L, _ = z.shape
    xbc, new_buf = causal_dwconv(xbc, conv_buf, conv_w, conv_b)
    xbc = jax.nn.silu(xbc.astype(f32))
    gn = SSD_GROUPS * SSD_STATE
    xs = xbc[..., :D_SSD].reshape(bsz, L, H_S, SSD_HEAD_DIM)
    Bm = xbc[..., D_SSD:D_SSD + gn].reshape(bsz, L, SSD_GROUPS, SSD_STATE)
    Cm = xbc[..., D_SSD + gn:].reshape(bsz, L, SSD_GROUPS, SSD_STATE)
    Bh = jnp.repeat(Bm, H_S // SSD_GROUPS, axis=2)
    Ch = jnp.repeat(Cm, H_S // SSD_GROUPS, axis=2)
    dt = jax.nn.softplus(dt_raw.astype(f32) + dt_bias.astype(f32))
    A = -jnp.exp(a_log.astype(f32))
    chunk = math.gcd(L, SSD_CHUNK)
    y, h_last = ssd_chunked(xs * dt[..., None], dt * A, Bh, Ch, h0.astype(f32), chunk)
    y = y + d_skip.astype(f32)[:, None] * xs
    y = y.reshape(bsz, L, D_SSD) * jax.nn.silu(z.astype(f32))
    return rms_norm(y, norm_g), new_buf, h_last


def trunk_layer(x, pos, win_k, win_v, ssd_buf, ssd_h0, ffn_buf,
                norm1_g, w_in, q_norm_g, k_norm_g, ssd_conv_w, ssd_conv_b,
                ssd_dt_bias, ssd_a_log, ssd_d, ssd_norm_g, w_out,
                norm2_g, w_up, ffn_conv_w, ffn_conv_b, w_down):
    proj = rms_norm(x, norm1_g) @ w_in
    q, k, v, z, xbc, dt_raw = jnp.split(proj, IN_SPLITS, axis=-1)
    att, k_rows, v_rows = attn_mixer(q, k, v, pos, q_norm_g, k_norm_g, win_k, win_v)
    ssd, ssd_buf_new, ssd_h = ssd_mixer(z, xbc, dt_raw, ssd_buf, ssd_h0, ssd_conv_w, ssd_conv_b,
                                        ssd_dt_bias, ssd_a_log, ssd_d, ssd_norm_g)
    mix = jnp.concatenate([att, ssd], axis=-1).astype(x.dtype)
    x = (x + mix @ w_out).astype(x.dtype)
    h = rms_norm(x, norm2_g) @ w_up
    h, ffn_buf_new = causal_dwconv(h, ffn_buf, ffn_conv_w, ffn_conv_b)
    gate, up = jnp.split(h, 2, axis=-1)
    x = (x + (jax.nn.silu(gate) * up) @ w_down).astype(x.dtype)
    return x, (k_rows, v_rows, ssd_buf_new, ssd_h, ffn_buf_new)


def setup_inputs(seed: int = 0) -> dict:
    key = jax.random.key(seed)
    ks = jax.random.split(key, 24)
    f32 = jnp.float32
    win_len = min(MAX_WINDOW, PAST_LEN)

    def nrm(k, shape, scale):
        return jax.random.normal(k, shape, f32) * scale

    dt0 = jnp.exp(jax.random.uniform(ks[13], (DEPTH, H_S), f32, math.log(1e-3), math.log(1e-1)))
    return {
        'x_prompt': nrm(ks[0], (BATCH, SEQ, D_MODEL), 1.0),
        'x_sample': nrm(ks[1], (DEC_BATCH, DEC_SEQ, D_MODEL), 1.0),
        'cache_win_k': nrm(ks[2], (DEPTH, DEC_BATCH, win_len, H_A, HD), 1.0),
        'cache_win_v': nrm(ks[3], (DEPTH, DEC_BATCH, win_len, H_A, HD), 1.0),
        'state_ssd_conv': nrm(ks[4], (DEPTH, DEC_BATCH, SSD_CONV - 1, D_XBC), 1.0),
        'state_ssd': nrm(ks[5], (DEPTH, DEC_BATCH, H_S, SSD_HEAD_DIM, SSD_STATE), 0.1),
        'state_ffn_conv': nrm(ks[6], (DEPTH, DEC_BATCH, FFN_CONV - 1, 2 * D_FF), 1.0),
        'norm1_g': 1.0 + nrm(ks[7], (DEPTH, D_MODEL), 0.02),
        'w_in': nrm(ks[8], (DEPTH, D_MODEL, D_IN_PROJ), D_MODEL ** -0.5),
        'q_norm_g': 1.0 + nrm(ks[9], (DEPTH, HD), 0.02),
        'k_norm_g': 1.0 + nrm(ks[10], (DEPTH, HD), 0.02),
        'ssd_conv_w': nrm(ks[11], (DEPTH, SSD_CONV, D_XBC), SSD_CONV ** -0.5),
        'ssd_conv_b': nrm(ks[12], (DEPTH, D_XBC), 0.02),
        'ssd_dt_bias': dt0 + jnp.log(-jnp.expm1(-dt0)),
        'ssd_a_log': jnp.log(jax.random.uniform(ks[14], (DEPTH, H_S), f32, 1.0, 16.0)),
        'ssd_d': 1.0 + nrm(ks[15], (DEPTH, H_S), 0.1),
        'ssd_norm_g': 1.0 + nrm(ks[16], (DEPTH, D_SSD), 0.02),
        'w_out': nrm(ks[17], (DEPTH, D_MIX, D_MODEL), D_MIX ** -0.5),
        'norm2_g': 1.0 + nrm(ks[18], (DEPTH, D_MODEL), 0.02),
        'w_up': nrm(ks[19], (DEPTH, D_MODEL, 2 * D_FF), D_MODEL ** -0.5),
        'ffn_conv_w': nrm(ks[20], (DEPTH, FFN_CONV, 2 * D_FF), 0.2).at[:, -1].add(1.0),
        'ffn_conv_b': nrm(ks[21], (DEPTH, 2 * D_FF), 0.02),
        'w_down': nrm(ks[22], (DEPTH, D_FF, D_MODEL), D_FF ** -0.5),
    }


def reference(x_prompt, x_sample, cache_win_k, cache_win_v, state_ssd_conv, state_ssd, state_ffn_conv,
              norm1_g, w_in, q_norm_g, k_norm_g, ssd_conv_w, ssd_conv_b, ssd_dt_bias, ssd_a_log,
              ssd_d, ssd_norm_g, w_out, norm2_g, w_up, ffn_conv_w, ffn_conv_b, w_down):
    bp, lp, _ = x_prompt.shape
    pos_p = jnp.arange(lp)
    pos_s = PAST_LEN + jnp.arange(x_sample.shape[1])
    zero_ssd_buf = jnp.zeros((bp, SSD_CONV - 1, D_XBC), x_prompt.dtype)
    zero_ssd_h = jnp.zeros((bp, H_S, SSD_HEAD_DIM, SSD_STATE), jnp.float32)
    zero_ffn_buf = jnp.zeros((bp, FFN_CONV - 1, 2 * D_FF), x_prompt.dtype)
    yp, ys = x_prompt, x_sample
    st_p, st_s = [], []
    for i in range(DEPTH):
        w_i = (norm1_g[i], w_in[i], q_norm_g[i], k_norm_g[i], ssd_conv_w[i], ssd_conv_b[i],
               ssd_dt_bias[i], ssd_a_log[i], ssd_d[i], ssd_norm_g[i], w_out[i],
               norm2_g[i], w_up[i], ffn_conv_w[i], ffn_conv_b[i], w_down[i])
        yp, sp = trunk_layer(yp, pos_p, None, None, zero_ssd_buf, zero_ssd_h, zero_ffn_buf, *w_i)
        ys, ss = trunk_layer(ys, pos_s, cache_win_k[i], cache_win_v[i], state_ssd_conv[i],
                             state_ssd[i], state_ffn_conv[i], *w_i)
        st_p.append(sp)
        st_s.append(ss)
    win_k_prompt = jnp.stack([s[0] for s in st_p])
    win_v_prompt = jnp.stack([s[1] for s in st_p])
    ssd_conv_prompt = jnp.stack([s[2] for s in st_p])
    ssd_state_prompt = jnp.stack([s[3] for s in st_p])
    ffn_conv_prompt = jnp.stack([s[4] for s in st_p])
    win_k_sample = jnp.stack([s[0] for s in st_s])
    win_v_sample = jnp.stack([s[1] for s in st_s])
    ssd_conv_sample = jnp.stack([s[2] for s in st_s])
    ssd_state_sample = jnp.stack([s[3] for s in st_s])
    ffn_conv_sample = jnp.stack([s[4] for s in st_s])
    return (yp, ys, win_k_prompt, win_v_prompt, win_k_sample, win_v_sample,
            ssd_conv_prompt, ssd_conv_sample, ssd_state_prompt, ssd_state_sample,
            ffn_conv_prompt, ffn_conv_sample)
```

```python
import contextlib
import numpy as np
import ml_dtypes
import concourse.bass as bass
import concourse.mybir as mybir
from concourse.bass_utils import run_bass_kernel_spmd

F32 = mybir.dt.float32
BF16 = mybir.dt.bfloat16
ALU = mybir.AluOpType
AF = mybir.ActivationFunctionType
bf16_np = ml_dtypes.bfloat16

DEPTH = 4
D = 2048
KC = 16
T = 1024
NS = 4
NH = 5
TT = T + NS + NH
TS = T + NS
CT = [(0, 512), (512, 512), (1024, NS + NH)]
D_IN = 6160
D_FF = 5632
NPC = 484
PAST = 16384
EPS = 1e-6
PC_G1, PC_G2, PC_SCW, PC_SCB, PC_FCW, PC_FCB, PC_QG, PC_KG, PC_SNG, PC_DCOL, PC_DTB, PC_ALOG = \
    0, 16, 32, 96, 112, 376, 464, 465, 466, 474, 482, 483


class Op:
    __slots__ = ("eng", "fn", "reads", "writes", "dma", "stream", "idx", "deps", "need_inc", "count", "cc")

    def __init__(self, eng, fn, reads, writes, dma=False, stream=None, cc=False):
        self.eng, self.fn, self.reads, self.writes = eng, fn, reads, writes
        self.dma, self.stream, self.cc = dma, stream, cc
        self.deps, self.need_inc, self.count = [], False, None


class Sched:
    def __init__(self, nc):
        self.nc = nc
        self.ops = []
        self.last_writer = {}
        self.readers = {}
        self.tag = None

    def _add(self, op):
        op.idx = len(self.ops)
        if self.tag is not None:
            op.reads = tuple(op.reads) + (self.tag,)
        deps = set()
        for r in op.reads:
            w = self.last_writer.get(r)
            if w is not None:
                deps.add(w)
            if r.startswith("ps"):
                for k_, i_ in self.readers.get(r, {}).items():
                    if k_ != op.eng:
                        deps.add(i_)
        for w_ in op.writes:
            w = self.last_writer.get(w_)
            if w is not None:
                deps.add(w)
            deps.update(self.readers.get(w_, {}).values())
        deps.discard(op.idx)
        op.deps = sorted(deps)
        rk = ("D", op.idx) if op.dma else op.eng
        for r in op.reads:
            self.readers.setdefault(r, {})[rk] = op.idx
        for w_ in op.writes:
            self.last_writer[w_] = op.idx
            self.readers[w_] = {}
        self.ops.append(op)
        return op

    def op(self, eng, fn, reads=(), writes=()):
        return self._add(Op(eng, fn, tuple(reads), tuple(writes)))

    def dma(self, queue, out, in_, reads=(), writes=(), stream=None):
        def fn(e, out=out, in_=in_):
            return e.dma_start(out=out, in_=in_)
        return self._add(Op(queue, fn, tuple(reads), tuple(writes), dma=True, stream=stream))

    def collective(self, fn, reads=(), writes=(), stream=None):
        return self._add(Op("pool", fn, tuple(reads), tuple(writes), dma=True, stream=stream, cc=True))

    def _skip(self, do, o):
        return (not do.dma) and (not o.dma) and do.eng == o.eng and do.eng == "pe"

    def emit(self, final_wait_engine="sp"):
        nc, ops = self.nc, self.ops
        for o in ops:
            if o.dma:
                o.need_inc = True
            for d in o.deps:
                do = ops[d]
                if not self._skip(do, o):
                    do.need_inc = True
        eng_cnt, stream_cnt = {}, {}
        for o in ops:
            if not o.need_inc:
                continue
            if o.dma:
                stream_cnt[o.stream] = stream_cnt.get(o.stream, 0) + (1 if o.cc else 16)
                o.count = stream_cnt[o.stream]
            else:
                eng_cnt[o.eng] = eng_cnt.get(o.eng, 0) + 1
                o.count = eng_cnt[o.eng]
        streams, engs = sorted(stream_cnt), sorted(eng_cnt)
        self.stats = dict(eng_cnt=eng_cnt, n_streams=len(streams), n_ops=len(ops))
        with contextlib.ExitStack() as es:
            sems = {}
            for e in engs:
                sems[("E", e)] = es.enter_context(nc.semaphore("p_" + e))
            for s in streams:
                sems[("S", s)] = es.enter_context(nc.semaphore("d_" + str(s)))
            block = es.enter_context(nc.Block())
            per_eng = {}
            for o in ops:
                per_eng.setdefault(o.eng, []).append(o)
            per_eng.setdefault(final_wait_engine, [])

            def run(eng_name, e):
                known = {}
                for o in per_eng[eng_name]:
                    need = {}
                    for d in o.deps:
                        do = ops[d]
                        if (not do.need_inc) or self._skip(do, o):
                            continue
                        k = ("S", do.stream) if do.dma else ("E", do.eng)
                        if do.count > need.get(k, 0):
                            need[k] = do.count
                    for k, v in need.items():
                        if known.get(k, 0) >= v:
                            continue
                        e.wait_ge(sems[k], v)
                        known[k] = v
                    ins = o.fn(e)
                    if o.need_inc:
                        if o.dma:
                            ins.then_inc(sems[("S", o.stream)], 1 if o.cc else 16)
                        else:
                            ins.then_inc(sems[("E", o.eng)], 1)
                if eng_name == final_wait_engine:
                    for s in streams:
                        e.wait_ge(sems[("S", s)], stream_cnt[s])
                    for en in engs:
                        e.wait_ge(sems[("E", en)], eng_cnt[en])

            if "pe" in per_eng:
                @block.tensor
                def _(e):
                    run("pe", e)
            if "act" in per_eng:
                @block.scalar
                def _(e):
                    run("act", e)
            if "dve" in per_eng:
                @block.vector
                def _(e):
                    run("dve", e)
            if "pool" in per_eng:
                @block.gpsimd
                def _(e):
                    run("pool", e)
            if "sp" in per_eng:
                @block.sync
                def _(e):
                    run("sp", e)


class Cfg:
    def __init__(self, n_layers=DEPTH, stop_after=None, dbg=False, no_cc=False):
        self.no_cc = no_cc
        self.n_layers = n_layers
        self.stop_after = stop_after
        self.dbg = dbg


class Builder:
    def __init__(self, cfg):
        self.cfg = cfg
        self.nc = nc = bass.Bass("TRN2", target_bir_lowering=False)
        self.S = Sched(nc)
        self.L = cfg.n_layers
        self.uid = 0
        self.decl_dram()
        self.decl_sbuf()

    def din(self, name, shape, dt=F32):
        return self.nc.dram_tensor(name, list(shape), dt, kind="ExternalInput").ap()

    def dout(self, name, shape, dt=F32):
        return self.nc.dram_tensor(name, list(shape), dt, kind="ExternalOutput").ap()

    def dint(self, name, shape, dt, dbg=False):
        kind = "ExternalOutput" if (dbg and self.cfg.dbg) else "Internal"
        return self.nc.dram_tensor(name, list(shape), dt, kind=kind).ap()

    def decl_dram(self):
        L = self.cfg.n_layers
        self.xT0 = self.din("xT0", [D, TT])
        self.w_in = self.din("w_in", [L, D, D_IN])
        self.w_out = self.din("w_out", [L, D, D])
        self.w_up = self.din("w_up", [L, D, 2 * D_FF])
        self.w_down = self.din("w_down", [L, D_FF, D])
        self.pcol_d = self.din("pcol", [128, DEPTH * NPC])
        self.cossin_d = self.din("cossin", [128, 2, TT])
        self.cbf_d = self.din("cbf", [128, self.cbf_layout()[1]], BF16)
        self.cf32_d = self.din("cf32", [128, 4 * 128 + 4])
        self.ksel = self.din("ksel", [L, 1024, 1152])
        self.vsel = self.din("vsel", [L, 1152, 1024])
        self.sconv_s = self.din("sconv_s", [L, 128, 16, 3])
        self.sstate_s = self.din("sstate_s", [L, 128, 1024])
        self.fconv_s = self.din("fconv_s", [L, 128, 88, 2])
        self.yT = self.dout("yT", [D, TT])
        self.kT_out = self.dout("kT_out", [L, 1024, TT])
        self.v_out = self.dout("v_out", [L, TT, 1024])
        self.sconv_out = self.dout("sconv_out", [L, 128, 16, 6])
        self.sstate_out = self.dout("sstate_out", [L, 2, 128, 1024])
        self.fconv_out = self.dout("fconv_out", [L, 128, 88, 4])
        self.xk_in = [self.dint(f"xk_in{l}", [1024, 1024], BF16) for l in range(L)]
        self.xk_out = [self.dint(f"xk_out{l}", [2048, 1024], BF16) for l in range(L)]
        self.xv_in = [self.dint(f"xv_in{l}", [1024, 1024], BF16) for l in range(L)]
        self.xv_out = [self.dint(f"xv_out{l}", [2048, 1024], BF16) for l in range(L)]
        self.xc_in = [self.dint(f"xc_in{l}", [128, 1024], F32) for l in range(L)]
        self.xc_out = [self.dint(f"xc_out{l}", [256, 1024], F32) for l in range(L)]
        self.xd_in = [self.dint(f"xd_in{l}", [KC, 128 * NH], F32) for l in range(L)]
        self.xd_out = [self.dint(f"xd_out{l}", [2 * KC, 128 * NH], F32) for l in range(L)]
        self.qT_s = self.dint("qT_s", [1024, TS], BF16, dbg=True)
        self.xbcT_s = self.dint("xbcT_s", [2048, TS], BF16, dbg=True)
        self.zT_s = self.dint("zT_s", [1024, TS], BF16, dbg=True)
        if self.cfg.dbg:
            self.mix_dbg = self.dout("mix_dbg", [128, 16, TT], BF16)
            self.x_dbg = self.dout("x_dbg", [D, TT])

    @staticmethod
    def cbf_layout():
        names = [("MUL", 512), ("MULf", 512), ("MULf1", 512), ("M16_0", 512), ("M16_1", 512),
                 ("IDB", 128), ("ONEB", 128), ("BD64", 128), ("EEXP", 1024), ("SM", 40), ("ZB", 128), ("RROTB", 128)]
        off, lay = 0, {}
        for n, w in names:
            lay[n] = (off, w)
            off += w
        return lay, off

    def sb(self, name, shape, dt):
        return self.nc.alloc_sbuf_tensor(name, list(shape), dt).ap()

    def decl_sbuf(self):
        nc = self.nc
        self.x = self.sb("x", [128, KC, TT], F32)
        self.xn = self.sb("xn", [128, KC, TT], BF16)
        self.wst = self.sb("wst", [128, 16384], BF16)
        self.pcol = self.sb("pcol_sb", [128, NPC], F32)
        self.cossin = self.sb("cossin_sb", [128, 2, TT], F32)
        lay, ncb = self.cbf_layout()
        self.cbf = self.sb("cbf_sb", [128, ncb], BF16)
        self.cb = {n: self.cbf[:, o:o + w] for n, (o, w) in lay.items()}
        self.cf32 = self.sb("cf32_sb", [128, 4 * 128 + 4], F32)
        self.IDF = self.cf32[:, 0:128]
        self.ONEF = self.cf32[:, 128:256]
        self.RROT = self.cf32[:, 256:384]
        self.UTRI = self.cf32[:, 384:512]
        self.FLAG = self.cf32[:, 512:513]
        self.EPSC = self.cf32[:, 513:514]
        self.dta = self.sb("dta", [16, 2, TS], F32)
        self.ksn = self.sb("ksn", [128, 8, NS], BF16)
        self.vsn = self.sb("vsn", [NS, 1024], BF16)
        self.NSCR = 48800
        self.scr = self.sb("scr", [128, self.NSCR // 4], F32)
        self.ps = [nc.alloc_psum_tensor(f"ps{i}", [128, 512], F32).ap() for i in range(8)]
        self.trip_i = 0

    def carve(self, off, shape, dt):
        esz = 2 if dt == BF16 else 4
        n = int(np.prod(shape[1:]))
        assert off % 4 == 0 and off + n * esz <= self.NSCR, (off, shape)
        nf = (n * esz + 3) // 4
        a = self.scr[0:shape[0], off // 4: off // 4 + nf]
        if dt == BF16:
            a = a.bitcast(BF16)[:, 0:n]
        if len(shape) == 2:
            return a
        names = " ".join(f"d{i}" for i in range(1, len(shape)))
        kw = {f"d{i}": shape[i] for i in range(1, len(shape))}
        return a.rearrange(f"p ({names}) -> p {names}", **kw)

    def pc(self, l, off, n=1):
        return self.pcol[:, off: off + n]

    def next_trip(self):
        t = (0, 1, 2) if self.trip_i % 2 == 0 else (3, 4, 5)
        self.trip_i += 1
        return t

    def A(self, out, in_, func, reads, writes, bias=None, scale=None):
        kw = {}
        if bias is not None:
            kw["bias"] = bias
        if scale is not None:
            kw["scale"] = scale
        self.S.op("act", lambda e: e.activation(out=out, in_=in_, func=func, **kw), reads, writes)

    def TTo(self, eng, out, in0, in1, op, reads, writes):
        self.S.op(eng, lambda e: e.tensor_tensor(out=out, in0=in0, in1=in1, op=op), reads, writes)

    def STT(self, out, in0, scalar, in1, op0, op1, reads, writes):
        self.S.op("dve", lambda e: e.scalar_tensor_tensor(out=out, in0=in0, scalar=scalar, in1=in1, op0=op0, op1=op1),
                  reads, writes)

    def TS(self, eng, out, in0, s1, op0, reads, writes, s2=None, op1=None):
        if op1 is None:
            self.S.op(eng, lambda e: e.tensor_scalar(out=out, in0=in0, scalar1=s1, scalar2=None, op0=op0), reads, writes)
        else:
            self.S.op(eng, lambda e: e.tensor_scalar(out=out, in0=in0, scalar1=s1, scalar2=s2, op0=op0, op1=op1), reads, writes)

    def CP(self, eng, out, in_, reads, writes):
        if eng == "act":
            self.S.op("act", lambda e: e.activation(out=out, in_=in_, func=AF.Copy), reads, writes)
        else:
            self.S.op(eng, lambda e: e.tensor_copy(out=out, in_=in_), reads, writes)

    def MM(self, out, lhsT, rhs, start, stop, reads, writes):
        self.S.op("pe", lambda e: e.matmul(out, lhsT=lhsT, rhs=rhs, start=start, stop=stop), reads, writes)

    def TR(self, out, in_, ident, reads, writes):
        self.S.op("pe", lambda e: e.transpose(out, in_, ident), reads, writes)

    def DMA(self, out, in_, reads, writes, q="sp", stream=None):
        if stream is None:
            self.uid += 1
            stream = f"u{self.uid % 24}"
        self.S.dma(q, out, in_, reads, writes, stream)

    def barrier(self):
        d = self.cf32[0:1, 515:516]
        self.S.tag = None
        self.S.op("act", lambda e: e.activation(out=d, in_=self.cf32[0:1, 514:515], func=AF.Copy), reads=["bar_d"], writes=["scr", "bar_d"])
        self.S.tag = "scr"

    def plan_weights(self):
        st = []
        for l in range(self.L):
            wi = self.w_in[l].rearrange("(kc p) n -> p kc n", p=128)
            for name, c0 in [("k", 1024), ("v", 2048), ("q", 0), ("z", 3072)]:
                for i in range(4):
                    st.append(((l, f"{name}{i}"), wi[:, :, c0 + 256 * i:c0 + 256 * i + 256], (KC, 256)))
            for i in range(8):
                st.append(((l, f"x{i}"), wi[:, :, 4096 + 256 * i:4096 + 256 * i + 256], (KC, 256)))
            st.append(((l, "dt"), wi[:, :, 6144:6160], (KC, 16)))
            wo = self.w_out[l].rearrange("(kc p) n -> p kc n", p=128)
            for i in range(8):
                st.append(((l, f"o{i}"), wo[:, :, 256 * i:256 * i + 256], (KC, 256)))
            wu = self.w_up[l].rearrange("(kc p) n -> p kc n", p=128)

            def unit(u):
                st.append(((l, f"g{u}"), wu[:, :, 256 * u:256 * u + 256], (KC, 256)))
                st.append(((l, f"u{u}"), wu[:, :, D_FF + 256 * u:D_FF + 256 * u + 256], (KC, 256)))

            def down(g):
                for h in range(2):
                    r0 = 512 * g + 256 * h
                    st.append(((l, f"d{2 * g + h}"), self.w_down[l][r0:r0 + 256, :].rearrange("(kc p) n -> p kc n", p=128), (2, 2048)))
            for u in range(22):
                unit(u)
                if u >= 2 and u % 2 == 0:
                    down(u // 2 - 1)
            down(10)
        self.wplan = st
        self.wpos = {k: i for i, (k, _, _) in enumerate(st)}
        self.wissued = 0
        self.wnext = 0

    def wview(self, slot, shp):
        a, b = shp
        return self.wst[:, 4096 * slot:4096 * slot + a * b].rearrange("p (a b) -> p a b", a=a)

    def issue_w(self, upto):
        tag, self.S.tag = self.S.tag, None
        while self.wissued <= min(upto, len(self.wplan) - 1):
            i = self.wissued
            key, src, shp = self.wplan[i]
            slot = i % 4
            self.S.dma("pool", self.wview(slot, shp), src, reads=[], writes=[f"wst{slot}"], stream=f"w{slot}")
            self.wissued += 1
        self.S.tag = tag

    def get_w(self, key):
        i = self.wpos[key]
        assert i == self.wnext, (key, i, self.wnext)
        self.wnext += 1
        self.issue_w(i + 2)
        _, _, shp = self.wplan[i]
        return self.wview(i % 4, shp), f"wst{i % 4}"

    def fm_tile(self, W, wkey, j0, M, rhs_of, nk, trip, rkeys):
        for kc in range(nk):
            for ti, (c0, n) in enumerate(CT):
                b = trip[ti]
                self.MM(self.ps[b][0:M, 0:n], W[:, kc, j0:j0 + M], rhs_of(kc)[:, c0:c0 + n], kc == 0, kc == nk - 1,
                        reads=[wkey, rkeys(kc)], writes=[f"ps{b}"])

    def load_consts(self):
        S = self.S
        self.DMA(self.cossin, self.cossin_d, [], ["cossin"])
        self.DMA(self.cbf, self.cbf_d, [], ["cbf"])
        self.DMA(self.cf32, self.cf32_d, [], ["cf32", "bar_d"])
        xv = self.xT0.rearrange("(kc p) t -> p kc t", p=128)
        for kc in range(KC):
            self.DMA(self.x[:, kc, :], xv[:, kc, :], [], [f"x{kc}"])
        S.op("pool", lambda e: e.memset(self.xn[:, :, TS:TT], 0.0), [], [f"xn{kc}" for kc in range(KC)])

    def norm_phase(self, l, goff, div):
        import os
        lvl = int(os.environ.get("N1LVL", "9"))
        sqb = [self.carve(0, [128, TT], BF16), self.carve(2080, [128, TT], BF16)]
        rstd = self.carve(4160, [128, TT], F32)
        trip = self.next_trip()
        for kc in range(KC):
            sq = sqb[kc % 2]
            if lvl >= 1:
                if kc % 2 == 0:
                    self.A(sq, self.x[:, kc, :], AF.Square, [f"x{kc}"], [f"sqb{kc % 2}"])
                else:
                    self.TTo("dve", sq, self.x[:, kc, :], self.x[:, kc, :], ALU.mult, [f"x{kc}"], [f"sqb{kc % 2}"])
            if lvl >= 2:
                for ti, (c0, n) in enumerate(CT):
                    self.MM(self.ps[trip[ti]][:, 0:n], self.cb["ONEB"], sq[:, c0:c0 + n], kc == 0, kc == KC - 1,
                            [f"sqb{kc % 2}", "cbf"], [f"ps{trip[ti]}"])
        if lvl >= 3:
            for ti, (c0, n) in enumerate(CT):
                self.A(rstd[:, c0:c0 + n], self.ps[trip[ti]][:, 0:n], AF.Sqrt, [f"ps{trip[ti]}", "cf32"], ["rstd"],
                       bias=self.EPSC, scale=1.0 / div)
        if lvl >= 4:
            self.S.op("dve", lambda e: e.reciprocal(out=rstd, in_=rstd), ["rstd"], ["rstd"])
        if lvl >= 5:
            for kc in range(KC):
                self.STT(self.xn[:, kc, :], self.x[:, kc, :], self.pc(l, goff + kc), rstd, ALU.mult, ALU.mult,
                         [f"x{kc}", "rstd", "pcol"], [f"xn{kc}"])

    def ip_phase(self, l):
        xn_of = lambda kc: self.xn[:, kc, :]
        xk = lambda kc: f"xn{kc}"
        o = 0
        def cv(shape, dt):
            nonlocal o
            esz = 2 if dt == BF16 else 4
            nb = (int(np.prod(shape[1:])) * esz + 31) // 32 * 32
            a_ = self.carve(o, shape, dt)
            o += nb
            return a_
        sqb = [cv([128, 1040], BF16) for _ in range(2)]
        t1 = [cv([128, 1040], F32) for _ in range(2)]
        t2 = [cv([128, 1040], F32) for _ in range(2)]
        t3 = [cv([128, 1040], F32) for _ in range(2)]
        kb = [cv([128, 1040], BF16) for _ in range(2)]
        t2b = [cv([128, 1040], BF16) for _ in range(2)]
        xgb = [cv([128, 1040], BF16) for _ in range(2)]
        vf = [cv([128, 256], F32) for _ in range(2)]
        vb = [cv([128, 256], BF16) for _ in range(2)]
        sst = cv([128, 16, 3], F32)
        sco = cv([128, 16, 6], F32)
        acol = cv([16, 1], F32)
        cos, sin = self.cossin[:, 0, :], self.cossin[:, 1, :]

        def qk_stage0(item):
            kind, c, slot = item[0], item[1], item[2]
            st, j = c // 2, c % 2
            if j == 0:
                self._qkW = self.get_w((l, f"{kind}{st}"))
            W, wkey = self._qkW
            trip = self.next_trip()
            item.append(trip)
            self.fm_tile(W, wkey, 128 * j, 128, xn_of, KC, trip, xk)

        def qk_stage0b(item):
            kind, c, slot, trip = item
            gcol = self.pc(l, PC_KG if kind == "k" else PC_QG)
            for ti, (c0, n) in enumerate(CT):
                self.A(sqb[slot][:, c0:c0 + n], self.ps[trip[ti]][:, 0:n], AF.Square, [f"ps{trip[ti]}"], [f"sqb{slot}"])
                self.TS("dve", xgb[slot][:, c0:c0 + n], self.ps[trip[ti]][:, 0:n], gcol, ALU.mult, [f"ps{trip[ti]}", "pcol"], [f"xgb{slot}"])

        def qk_stage1(item):
            kind, c, slot, trip = item
            for ti, (c0, n) in enumerate(CT):
                mb = 6 + (ti % 2)
                self.MM(self.ps[mb][:, 0:n], self.cb["BD64"], sqb[slot][:, c0:c0 + n], True, True, [f"sqb{slot}", "cbf"], [f"ps{mb}"])
                self.A(t1[slot][:, c0:c0 + n], self.ps[mb][:, 0:n], AF.Sqrt, [f"ps{mb}", "cf32"], [f"t1{slot}"], bias=self.EPSC, scale=1.0)
            self.S.op("dve", lambda e: e.reciprocal(out=t1[slot][:, 0:TT], in_=t1[slot][:, 0:TT]), [f"t1{slot}"], [f"t1{slot}"])
            self.TTo("pool", t2b[slot][:, 0:TT], xgb[slot][:, 0:TT], t1[slot][:, 0:TT], ALU.mult, [f"xgb{slot}", f"t1{slot}"], [f"t2b{slot}"])

        def qk_stage2(item):
            kind, c, slot, trip = item
            self.TTo("pool", t3[slot][:, 0:TT], t2b[slot][:, 0:TT], cos, ALU.mult, [f"t2b{slot}", "cossin"], [f"t3{slot}"])
            rsh = sqb[slot]
            for qd, (dst, src) in enumerate([(0, 32), (32, 0), (64, 96), (96, 64)]):
                self.CP("act" if qd % 2 == 0 else "pool", rsh[dst:dst + 32, 0:TT], t2b[slot][src:src + 32, 0:TT], [f"t2b{slot}"], [f"sqb{slot}"])
            self.TTo("dve", t1[slot][:, 0:TT], rsh[:, 0:TT], sin, ALU.mult, [f"sqb{slot}", "cossin"], [f"t1{slot}"])
            if kind == "k":
                self.TTo("pool", t3[slot][:, 0:TT], t3[slot][:, 0:TT], t1[slot][:, 0:TT], ALU.add, [f"t3{slot}", f"t1{slot}"], [f"t3{slot}"])
                self.DMA(self.kT_out[l, 128 * c:128 * c + 128, :], t3[slot][:, 0:TT], [f"t3{slot}"], [])
                self.CP("act", kb[slot][:, 0:TS], t3[slot][:, 0:TS], [f"t3{slot}"], [f"kb{slot}"])
                self.DMA(self.xk_in[l][128 * c:128 * c + 128, :], kb[slot][:, 0:T], [f"kb{slot}"], [f"xi{l}.k{c}"])
                self.CP("pool", self.ksn[:, c, :], kb[slot][:, T:TS], [f"kb{slot}"], [f"ksn{c}"])
            else:
                self.TTo("pool", kb[slot][:, 0:TS], t3[slot][:, 0:TS], t1[slot][:, 0:TS], ALU.add, [f"t3{slot}", f"t1{slot}"], [f"kb{slot}"])
                self.DMA(self.qT_s[128 * c:128 * c + 128, :], kb[slot][:, 0:TS], [f"kb{slot}"], [f"qT{c}"])

        def run_pipe(items, stages):
            n = len(items)
            for step in range(n + 2):
                if step < n:
                    qk_stage0(items[step])
                if 0 <= step - 1 < n:
                    qk_stage1(items[step - 1])
                if 0 <= step - 2 < n:
                    qk_stage2(items[step - 2])
                if step < n:
                    qk_stage0b(items[step])

        import os
        lvl = int(os.environ.get("IPLVL", "9"))
        nst = int(os.environ.get("IPNST", "3"))
        run_pipe([["k", c, c % 2] for c in range(8)], [qk_stage0, qk_stage1, qk_stage2][:nst])
        if lvl < 2:
            return
        banks = [0, 1, 2, 3, 4, 5]
        cnt = 0
        for st in range(4):
            W, wkey = self.get_w((l, f"v{st}"))
            for tj in range(int(os.environ.get("VNT", "9"))):
                r0, m = (128 * tj, 128) if tj < 8 else (T, NS + NH)
                b = banks[cnt % 6]
                sl = cnt % 2
                cnt += 1
                for kc in range(KC):
                    self.MM(self.ps[b][0:m, 0:256], self.xn[:, kc, r0:r0 + m], W[:, kc, :], kc == 0, kc == KC - 1,
                            [wkey, f"xn{kc}"], [f"ps{b}"])
                vm = os.environ.get("VMODE", "ab")
                if "a" in vm:
                    self.CP("act", vf[sl][0:m, :], self.ps[b][0:m, 0:256], [f"ps{b}"], [f"vf{sl}"])
                    self.DMA(self.v_out[l, r0:r0 + m, 256 * st:256 * st + 256], vf[sl][0:m, :], [f"vf{sl}"], [])
                if "b" in vm:
                    self.CP("dve", vb[sl][0:m, :], vf[sl][0:m, :], [f"vf{sl}"], [f"vb{sl}"])
                    if tj < 8:
                        self.DMA(self.xv_in[l][r0:r0 + 128, 256 * st:256 * st + 256], vb[sl], [f"vb{sl}"], [f"xi{l}.v{st}.{tj}"])
                    else:
                        self.CP("pool", self.vsn[:, 256 * st:256 * st + 256], vb[sl][0:NS, :], [f"vb{sl}"], [f"vsn{st}"])
        self.exchange(self.xk_in[l], self.xk_out[l], self.xi_keys(l)[:8], [f"xok{l}"], f"ccBk{l}")
        self.exchange(self.xv_in[l], self.xv_out[l], self.xi_keys(l)[8:], [f"xov{l}"], f"ccBv{l}")
        if lvl < 3:
            return
        run_pipe([["q", c, c % 2] for c in range(8)], [qk_stage0, qk_stage1, qk_stage2])
        if lvl < 4:
            return
        for st in range(4):
            W, wkey = self.get_w((l, f"z{st}"))
            for j in range(2):
                c = 2 * st + j
                sl = c % 2
                trip = self.next_trip()
                self.fm_tile(W, wkey, 128 * j, 128, xn_of, KC, trip, xk)
                for ti, (c0, n) in enumerate(CT):
                    self.CP("act" if ti == 0 else "dve", kb[sl][:, c0:c0 + n], self.ps[trip[ti]][:, 0:n], [f"ps{trip[ti]}"], [f"kb{sl}"])
                self.DMA(self.zT_s[128 * c:128 * c + 128, :], kb[sl][:, 0:TS], [f"kb{sl}"], [f"zT{c}"])
        if lvl < 5:
            return
        XL = 3 + T + 3 + NS + NH
        NO = XL - 3
        self.DMA(sst, self.sconv_s[l], [], ["sst"])
        for st in range(8):
            W, wkey = self.get_w((l, f"x{st}"))
            for j in range(2):
                c = 2 * st + j
                sl = c % 2
                trip = self.next_trip()
                self.fm_tile(W, wkey, 128 * j, 128, xn_of, KC, trip, xk)
                X, acc, xa = t1[sl], t2[sl], kb[sl]
                rk = [f"t1{sl}"]
                self.CP("act", X[:, 3:515], self.ps[trip[0]][:, 0:512], [f"ps{trip[0]}"], rk)
                self.CP("act", X[:, 515:1027], self.ps[trip[1]][:, 0:512], [f"ps{trip[1]}"], rk)
                self.CP("dve", X[:, 1030:1039], self.ps[trip[2]][:, 0:9], [f"ps{trip[2]}"], rk)
                self.CP("pool", X[:, 1027:1030], sst[:, c, :], ["sst"] + rk, rk)
                self.TS("dve", X[:, 0:3], X[:, 1036:1039], self.FLAG, ALU.mult, rk + ["cf32"], rk)
                self.CP("pool", sco[:, c, 0:3], X[:, 1024:1027], rk, ["sco"])
                self.CP("pool", sco[:, c, 3:6], X[:, 1031:1034], rk, ["sco"])
                wcol = lambda t: self.pc(l, PC_SCW + 16 * t + c)
                ak = [f"t2{sl}"]
                self.A(acc[:, 0:NO], X[:, 3:3 + NO], AF.Identity, rk + ["pcol"], ak, bias=self.pc(l, PC_SCB + c), scale=wcol(3))
                for t in (2, 1, 0):
                    self.STT(acc[:, 0:NO], X[:, t:t + NO], wcol(t), acc[:, 0:NO], ALU.mult, ALU.add, rk + ak + ["pcol"], ak)
                self.A(xa[:, 0:NO], acc[:, 0:NO], AF.Silu, ak, [f"kb{sl}"])
                self.DMA(self.xbcT_s[128 * c:128 * c + 128, 0:T], xa[:, 0:T], [f"kb{sl}"], [f"xbcT{c}"])
                self.DMA(self.xbcT_s[128 * c:128 * c + 128, T:TS], xa[:, 1027:1031], [f"kb{sl}"], [f"xbcT{c}"])
        self.DMA(self.sconv_out[l], sco, ["sco"], [])
        if lvl < 6:
            return
        W, wkey = self.get_w((l, "dt"))
        trip = self.next_trip()
        self.fm_tile(W, wkey, 0, 16, xn_of, KC, trip, xk)
        dt, av = self.dta[:, 0, :], self.dta[:, 1, :]
        for ti, (c0, n) in enumerate(CT):
            n2 = min(n, TS - c0)
            self.A(dt[:, c0:c0 + n2], self.ps[trip[ti]][0:16, 0:n2], AF.Exp, [f"ps{trip[ti]}", "pcol"], ["dta"],
                   bias=self.pcol[0:16, PC_DTB:PC_DTB + 1], scale=1.0)
        self.A(dt, dt, AF.Ln, ["dta"], ["dta"], bias=1.0, scale=1.0)
        self.A(acol, self.pcol[0:16, PC_ALOG:PC_ALOG + 1], AF.Exp, ["pcol"], ["acol"])
        self.TS("dve", av, dt, acol, ALU.mult, ["dta", "acol"], ["dta"], s2=-1.0, op1=ALU.mult)

    def xi_keys(self, l):
        return [f"xi{l}.k{c}" for c in range(8)] + [f"xi{l}.v{st}.{tj}" for st in range(4) for tj in range(8)]

    def exchange(self, src, dst, rkeys, wkeys, stream):
        if self.cfg.no_cc:
            self.DMA(dst[0:src.shape[0], :], src, rkeys, wkeys)
            return
        rg = [[0, 1], [2, 3], [4, 5], [6, 7]]
        self.S.collective(lambda e: e.collective_compute("AllGather", ALU.bypass, replica_groups=rg, ins=[src], outs=[dst]),
                          reads=rkeys, writes=wkeys, stream=stream)

    def ssd_carve(self):
        o = 0
        c = {}
        def add(name, shape, dt):
            nonlocal o
            esz = 2 if dt == BF16 else 4
            nb = int(np.prod(shape[1:])) * esz
            nb = (nb + 31) // 32 * 32
            c[name] = self.carve(o, shape, dt)
            o += nb
        add("hT", [128, 16, 64], F32)
        add("hTs", [128, 16, 64], F32)
        add("acum", [16, TS], F32)
        self.ssd_persist = o
        add("xbc", [128, 16, 128], BF16)
        add("xstm", [128, 1024], BF16)
        add("btm", [128, 512], BF16)
        add("dtm", [128, 32], F32)
        add("acm", [128, 16], F32)
        add("XP", [128, 16, 128], BF16)
        add("XDD", [128, 16, 64], BF16)
        for q in range(2):
            add(f"X{q}", [128, 4, 128], F32)
            add(f"SEG{q}", [128, 4, 128], F32)
            add(f"EC{q}", [128, 4, 128], F32)
        add("WT", [128, 4, 128], BF16)
        add("CS", [128, 4, 128], BF16)
        add("HP", [128, 16, 128], BF16)
        assert o <= self.NSCR, o
        return c

    def ssd_local(self, l):
        c = self.ssd_carve()
        S = self.S
        hT, hTs, acum = c["hT"], c["hTs"], c["acum"]
        xbc, xstm, btm, dtm, acm = c["xbc"], c["xstm"], c["btm"], c["dtm"], c["acm"]
        XP, XDD, WT, CS, HP = c["XP"], c["XDD"], c["WT"], c["CS"], c["HP"]
        X2, SEG2, EC2 = [c["X0"], c["X1"]], [c["SEG0"], c["SEG1"]], [c["EC0"], c["EC1"]]
        S.op("pool", lambda e: e.memset(XP, 0.0), [], ["XP"])
        S.op("pool", lambda e: e.memset(HP, 0.0), [], ["HP"])
        psT = self.ps[0].bitcast(BF16)
        psT2 = self.ps[1].bitcast(BF16)
        IDB = self.cb["IDB"]
        for ch in range(9):
            c0, Lc = (128 * ch, 128) if ch < 8 else (T, NS)
            first = ch == 0 or ch == 8
            self.DMA(xbc[:, :, 0:Lc], self.xbcT_s.rearrange("(kc p) t -> p kc t", p=128)[:, :, c0:c0 + Lc],
                     [f"xbcT{k}" for k in range(16)], ["xbc"])
            for kc in range(8):
                self.TR(psT[0:Lc, 128 * kc:128 * kc + 128], xbc[:, kc, 0:Lc], IDB, ["xbc", "cbf"], ["ps0"])
            self.CP("act", xstm[0:Lc, :], psT[0:Lc, :], ["ps0"], ["xstm"])
            for kc in range(4):
                self.TR(psT2[0:Lc, 128 * kc:128 * kc + 128], xbc[:, 8 + kc, 0:Lc], IDB, ["xbc", "cbf"], ["ps1"])
            self.CP("act", btm[0:Lc, :], psT2[0:Lc, 0:512], ["ps1"], ["btm"])
            for k in range(2):
                self.TR(self.ps[2][0:Lc, 16 * k:16 * k + 16], self.dta[:, k, c0:c0 + Lc], self.IDF[0:16, 0:16], ["dta", "cf32"], ["ps2"])
            self.CP("dve", dtm[0:Lc, :], self.ps[2][0:Lc, 0:32], ["ps2"], ["dtm"])
            self.MM(self.ps[2][0:Lc, 32:48], self.UTRI[0:Lc, 0:Lc], dtm[0:Lc, 16:32], True, True, ["dtm", "cf32"], ["ps2"])
            self.CP("dve", acm[0:Lc, :], self.ps[2][0:Lc, 32:48], ["ps2"], ["acm"])
            self.MM(self.ps[2][0:16, 64:64 + Lc], dtm[0:Lc, 16:32], self.UTRI[0:Lc, 0:Lc], True, True, ["dtm", "cf32"], ["ps2"])
            if first:
                self.CP("dve", acum[:, c0:c0 + Lc], self.ps[2][0:16, 64:64 + Lc], ["ps2"], ["acum"])
            else:
                self.TS("dve", acum[:, c0:c0 + Lc], self.ps[2][0:16, 64:64 + Lc], acum[:, c0 - 1:c0], ALU.add, ["ps2", "acum"], ["acum"])
            def grp_bufs(g):
                q = g % 2
                return (X2[q], SEG2[q], EC2[q], f"X{q}", f"SEG{q}", f"EC{q}")

            def partA(g):
                X, SEG, EC, Xk, SEGk, ECk = grp_bufs(g)
                a4 = dtm[0:Lc, 16 + 4 * g:20 + 4 * g]
                d4 = dtm[0:Lc, 4 * g:4 * g + 4]
                Xv, SEGv, ECv = X[0:Lc, :, 0:Lc], SEG[0:Lc, :, 0:Lc], EC[:, :, 0:Lc]
                U3 = self.UTRI[0:Lc, 0:Lc].unsqueeze(1).to_broadcast([Lc, 4, Lc])
                self.TTo("dve", Xv, U3, a4.unsqueeze(2).to_broadcast([Lc, 4, Lc]), ALU.mult, ["dtm", "cf32"], [Xk])
                psA = self.ps[3][:, 0:4 * Lc].rearrange("p (h i) -> p h i", h=4)
                if Lc == 128:
                    self.MM(self.ps[3][:, 0:512], self.ONEF[0:Lc, :], X.rearrange("p h i -> p (h i)"), True, True, [Xk, "cf32"], ["ps3"])
                else:
                    for h in range(4):
                        self.MM(self.ps[3][:, Lc * h:Lc * h + Lc], self.ONEF[0:Lc, :], X[0:Lc, h, 0:Lc], True, True, [Xk, "cf32"], ["ps3"])
                self.TTo("dve", SEGv, psA[0:Lc], acm[0:Lc, 4 * g:4 * g + 4].unsqueeze(2).to_broadcast([Lc, 4, Lc]), ALU.subtract,
                         ["ps3", "acm"], [SEGk])
                self.A(SEGv, SEGv, AF.Exp, [SEGk], [SEGk])
                self.STT(SEGv, SEGv, 1.0, U3, ALU.min, ALU.mult, [SEGk, "cf32"], [SEGk])
                self.A(ECv, psA, AF.Exp, ["ps3"], [ECk])
            def partB(g):
                X, SEG, EC, Xk, SEGk, ECk = grp_bufs(g)
                a4 = dtm[0:Lc, 16 + 4 * g:20 + 4 * g]
                d4 = dtm[0:Lc, 4 * g:4 * g + 4]
                Xv, SEGv, ECv = X[0:Lc, :, 0:Lc], SEG[0:Lc, :, 0:Lc], EC[:, :, 0:Lc]
                psC = self.ps[2][0:Lc, 256:256 + Lc]
                self.MM(psC, xbc[:, 8 + g, 0:Lc], xbc[:, 12 + g, 0:Lc], True, True, ["xbc"], ["ps2"])
                self.TTo("dve", WT[0:Lc, :, 0:Lc], SEGv, psC.unsqueeze(1).to_broadcast([Lc, 4, Lc]), ALU.mult, [SEGk, "ps2"], ["WT"])
                if not first:
                    self.TTo("pool", CS[:, :, 0:Lc], ECv, xbc[:, 12 + g, 0:Lc].unsqueeze(1).to_broadcast([128, 4, Lc]), ALU.mult,
                             [ECk, "xbc"], ["CS"])
                xs4 = xstm[0:Lc, 256 * g:256 * g + 256].rearrange("p (u s d) -> p u s d", u=2, s=2)
                d44 = d4.rearrange("p (u s) -> p u s", u=2)
                for s_ in range(2):
                    self.TTo("dve", XP[0:Lc, 4 * g + s_:4 * g + 4:2, 64 * s_:64 * s_ + 64], xs4[:, :, s_, :],
                             d44[:, :, s_].unsqueeze(2).to_broadcast([Lc, 2, 64]), ALU.mult, ["xstm", "dtm"], ["XP"])
                xdd_o = XDD[0:Lc, 4 * g:4 * g + 4, :]
                dend = SEG[0:Lc, :, Lc - 1:Lc].to_broadcast([Lc, 4, 64])
                self.TTo("dve", xdd_o, xstm[0:Lc, 256 * g:256 * g + 256].rearrange("p (h d) -> p h d", h=4),
                         d4.unsqueeze(2).to_broadcast([Lc, 4, 64]), ALU.mult, ["xstm", "dtm"], ["XDD"])
                self.TTo("dve", xdd_o, xdd_o, dend, ALU.mult, ["XDD", SEGk], ["XDD"])
                for u in range(2):
                    pr = 2 * g + u
                    yb = 4 + pr // 4
                    yo = self.ps[yb][:, Lc * (pr % 4):Lc * (pr % 4) + Lc]
                    nmm = 2 if first else 4
                    k = 0
                    for s in range(2):
                        h = 2 * u + s
                        self.MM(yo, XP[0:Lc, 4 * g + h, :], WT[0:Lc, h, 0:Lc], k == 0, k == nmm - 1, ["XP", "WT"], [f"ps{yb}"])
                        k += 1
                    if not first:
                        for s in range(2):
                            h = 2 * u + s
                            self.MM(yo, HP[:, 4 * g + h, :], CS[:, h, 0:Lc], False, k == nmm - 1, ["HP", "CS"], [f"ps{yb}"])
                            k += 1
                sb_ = 6 + g // 2
                for h in range(4):
                    so = self.ps[sb_][:, 64 * (4 * (g % 2) + h):64 * (4 * (g % 2) + h) + 64]
                    self.MM(so, btm[0:Lc, 128 * g:128 * g + 128], XDD[0:Lc, 4 * g + h, :], True, True, ["btm", "XDD"], [f"ps{sb_}"])
                sv = self.ps[sb_][:, 256 * (g % 2):256 * (g % 2) + 256].rearrange("p (h d) -> p h d", h=4)
                if ch == 8:
                    self.CP("act", hTs[:, 4 * g:4 * g + 4, :], sv, [f"ps{sb_}"], ["hTs"])
                elif ch == 0:
                    self.CP("act", hT[:, 4 * g:4 * g + 4, :], sv, [f"ps{sb_}"], ["hT"])
                else:
                    hv = hT[:, 4 * g:4 * g + 4, :]
                    self.TTo("dve", hv, hv, EC[:, :, Lc - 1:Lc].to_broadcast([128, 4, 64]), ALU.mult, ["hT", ECk, "HP"], ["hT"])
                    self.TTo("dve", hv, hv, sv, ALU.add, ["hT", f"ps{sb_}"], ["hT"])
            for step in range(5):
                if step < 4:
                    partA(step)
                if step >= 1:
                    partB(step - 1)
            for pr in range(8):
                yb = 4 + pr // 4
                yo = self.ps[yb][:, Lc * (pr % 4):Lc * (pr % 4) + Lc]
                self.STT(self.xn[:, 8 + pr, c0:c0 + Lc], xbc[:, pr, 0:Lc], self.pc(l, PC_DCOL + pr), yo, ALU.mult, ALU.add,
                         ["xbc", f"ps{yb}", "pcol"], [f"xn{8 + pr}"])
            if ch < 7:
                for s_ in range(2):
                    self.CP("act", HP[:, s_:16:2, 64 * s_:64 * s_ + 64], hT[:, s_:16:2, :], ["hT"], ["HP"])
        self.DMA(self.xc_in[l], hT.rearrange("p h d -> p (h d)"), ["hT"], [f"xci{l}"])
        self.exchange(self.xc_in[l], self.xc_out[l], [f"xci{l}"], [f"xco{l}"], f"ccC{l}")

    def att_phase(self, l):
        base = self.ssd_persist
        o = base
        def cv(shape, dt):
            nonlocal o
            esz = 2 if dt == BF16 else 4
            nb = (int(np.prod(shape[1:])) * esz + 31) // 32 * 32
            a = self.carve(o, shape, dt)
            o += nb
            return a
        KT2 = [cv([128, 2048], BF16) for _ in range(2)]
        QT2 = [cv([128, TS], BF16) for _ in range(2)]
        V1 = cv([128, 9, 128], BF16)
        V4 = cv([128, 4, 3, 128], BF16)
        V16 = cv([128, 16, 128], BF16)
        KS = cv([128, 1152], BF16)
        VS = cv([128, 9, 128], BF16)
        NPE, NPP, DEPTH_PIPE = 3, 4, 3
        PE_ = [cv([128, 512], BF16) for _ in range(NPE)]
        PP = [cv([128, 512], BF16) for _ in range(NPP)]
        RD = cv([128, 512], F32)
        kin, kout, vin, vout = self.xk_in[l], self.xk_out[l], self.xv_in[l], self.xv_out[l]
        xi_keys = self.xi_keys(l)
        kk, vk = xi_keys[:8], xi_keys[8:]
        ONEB = self.cb["ONEB"]
        jobs = []

        def add_job(scores, post, pv, pre=None, fin=None, ld=None):
            jobs.append(dict(scores=scores, post=post, pv=pv, pre=pre, fin=fin, ld=ld))

        def loads_kq(c):
            KT, QT, q = KT2[c % 2], QT2[c % 2], c % 2
            self.DMA(KT[:, 0:1024], kout[128 * c:128 * c + 128, :], [f"xok{l}"], [f"KT{q}"])
            self.DMA(KT[:, 1024:2048], kin[128 * c:128 * c + 128, :], kk, [f"KT{q}"])
            self.DMA(QT, self.qT_s[128 * c:128 * c + 128, :], [f"qT{c}"], [f"QT{q}"])

        def loads_v(c):
            fs = slice(128 * c, 128 * c + 128)
            self.DMA(V1[:, 0, :], vout[896:1024, fs], [f"xov{l}"], ["V1"])
            self.DMA(V1[:, 1:9, :], vin[:, fs].rearrange("(b p) f -> p b f", p=128), vk, ["V1"])
            self.DMA(V4[:, :, 0, :], vout[512:1024, fs].rearrange("(i r) f -> i r f", r=4), [f"xov{l}"], ["V4"])
            for cb_ in range(2):
                self.DMA(V4[:, :, 1 + cb_, :], vin[512 * cb_:512 * cb_ + 512, fs].rearrange("(i r) f -> i r f", r=4), vk, ["V4"])
            self.DMA(V16[0:64], vout[0:1024, fs].rearrange("(m r) f -> m r f", r=16), [f"xov{l}"], ["V16"])
            self.DMA(V16[64:128], vin[:, fs].rearrange("(m r) f -> m r f", r=16), vk, ["V16"])
            self.DMA(KS, self.ksel[l, 128 * c:128 * c + 128, :], [], ["KS"], q="pool", stream="ks")
            self.DMA(VS, self.vsel[l, :, fs].rearrange("(s k) f -> k s f", k=128), [], ["VS"], q="pool", stream="vs")

        ji = [0]

        def make_batch(c, tt, s, mk, tl, first, last, is_first_of_pair):
            ktk, qtk = f"KT{c % 2}", f"QT{c % 2}"
            i = ji[0]
            ji[0] += 1
            sb_ = 4 + (i % 4)
            pe_, pek = PE_[i % NPE], f"PE{i % NPE}"
            P_, pk = PP[i % NPP], f"PP{i % NPP}"
            pr = slice(64 * s, 64 * s + 64)
            NUM, DEN = self.ps[s], self.ps[2 + s]
            nk, dk = f"ps{s}", f"ps{2 + s}"

            def scores():
                off = 0
                for (ka, qa, nq, vt, oc) in tl:
                    self.MM(self.ps[sb_][:, off:off + nq], ka, qa, True, True, [ktk, qtk], [f"ps{sb_}"])
                    off += nq

            def post():
                self.A(pe_, self.ps[sb_], AF.Exp, [f"ps{sb_}"], [pek], scale=0.125)
                self.TTo("dve", P_, pe_, self.cb[mk], ALU.mult, [pek, "cbf"], [pk])

            def pre():
                self.MM(NUM, self.cb["ZB"], self.cb["MUL"], True, False, ["cbf"], [nk])
                self.MM(DEN, self.cb["ZB"], self.cb["MUL"], True, False, ["cbf"], [dk])

            def pv():
                off = 0
                for (ka, qa, nq, vt, oc) in tl:
                    self.MM(oc(NUM)[pr], vt[:, pr], P_[:, off:off + nq], False, False, [pk, "V1", "V4", "V16"], [nk])
                    self.MM(oc(DEN)[pr], ONEB[:, 0:64], P_[:, off:off + nq], False, False, [pk, "cbf"], [dk])
                    off += nq

            def fin():
                self.S.op("dve", lambda e: e.reciprocal(out=RD[pr, :], in_=DEN[pr, :]), [dk], ["RD"])
                self.TTo("dve", self.xn[pr, c, 512 * tt:512 * tt + 512], NUM[pr, :], RD[pr, :], ALU.mult, [nk, "RD"], [f"xn{c}"])

            add_job(scores, post, pv, pre if first else None, fin if last else None, c if is_first_of_pair else None)

        def make_sample(c, s):
            i = ji[0]
            ji[0] += 1
            sb_ = 4 + (i % 4)
            pe_, pek = PE_[i % NPE], f"PE{i % NPE}"
            P_, pk = PP[i % NPP], f"PP{i % NPP}"
            pr = slice(64 * s, 64 * s + 64)
            NUM, DEN = self.ps[s], self.ps[2 + s]
            nk, dk = f"ps{s}", f"ps{2 + s}"
            qa = QT2[c % 2][pr, T:TS]
            qtk = f"QT{c % 2}"

            def scores():
                for st_ in range(9):
                    self.MM(self.ps[sb_][:, 4 * st_:4 * st_ + 4], KS[pr, 128 * st_:128 * st_ + 128], qa, True, True, ["KS", qtk], [f"ps{sb_}"])
                self.MM(self.ps[sb_][0:NS, 36:40], self.ksn[pr, c, :], qa, True, True, [f"ksn{c}", qtk], [f"ps{sb_}"])

            def post():
                self.A(pe_[:, 0:36], self.ps[sb_][:, 0:36], AF.Exp, [f"ps{sb_}"], [pek], scale=0.125)
                self.A(pe_[0:NS, 36:40], self.ps[sb_][0:NS, 36:40], AF.Exp, [f"ps{sb_}"], [pek], scale=0.125)
                self.TTo("dve", P_[:, 0:36], pe_[:, 0:36], self.cb["SM"][:, 0:36], ALU.mult, [pek, "cbf"], [pk])
                self.TTo("dve", P_[0:NS, 36:40], pe_[0:NS, 36:40], self.cb["SM"][0:NS, 36:40], ALU.mult, [pek, "cbf"], [pk])

            def pv():
                for st_ in range(9):
                    self.MM(NUM[pr, 0:NS], VS[:, st_, pr], P_[:, 4 * st_:4 * st_ + 4], st_ == 0, False, [pk, "VS"], [nk])
                    self.MM(DEN[pr, 0:NS], ONEB[:, 0:64], P_[:, 4 * st_:4 * st_ + 4], st_ == 0, False, [pk, "cbf"], [dk])
                self.MM(NUM[pr, 0:NS], self.vsn[:, 128 * c + 64 * s:128 * c + 64 * s + 64], P_[0:NS, 36:40], False, True, [pk] + [f"vsn{i_}" for i_ in range(4)], [nk])
                self.MM(DEN[pr, 0:NS], ONEB[0:NS, 0:64], P_[0:NS, 36:40], False, True, [pk, "cbf"], [dk])

            def fin():
                self.S.op("dve", lambda e: e.reciprocal(out=RD[pr, 0:NS], in_=DEN[pr, 0:NS]), [dk], ["RD"])
                self.TTo("dve", self.xn[pr, c, T:TS], NUM[pr, 0:NS], RD[pr, 0:NS], ALU.mult, [nk, "RD"], [f"xn{c}"])

            add_job(scores, post, pv, None, fin)
            jobs[-1]["last_of_pair"] = (s == 1)
            jobs[-1]["c"] = c

        for c in range(8):
            KT, QT = KT2[c % 2], QT2[c % 2]
            first_of_pair = True
            for tt in range(2):
                for s in range(2):
                    pr = slice(64 * s, 64 * s + 64)
                    batches = []
                    for half in range(2):
                        tl = []
                        for qb in (2 * half, 2 * half + 1):
                            cq = 8 + 4 * tt + qb
                            qa = QT[pr, 512 * tt + 128 * qb:512 * tt + 128 * qb + 128]
                            oc = lambda P, qb=qb: P[:, 128 * qb:128 * qb + 128]
                            tl.append((KT[pr, 128 * (cq - 1):128 * cq], qa, 128, V1[:, cq - 1 - 7, :], oc))
                            tl.append((KT[pr, 128 * cq:128 * cq + 128], qa, 128, V1[:, cq - 7, :], oc))
                        batches.append(("MULf1" if (tt == 0 and half == 0) else "MUL", tl))
                    for half in range(2):
                        tl = []
                        cq = 2 + tt
                        for r in (2 * half, 2 * half + 1):
                            qa = QT[pr, 512 * tt + r:512 * tt + 512:4]
                            oc = lambda P, r=r: P[:, r:512:4]
                            for cbk in (cq - 1, cq):
                                tl.append((KT[pr, 512 * cbk + r:512 * cbk + 512:4], qa, 128, V4[:, r, cbk - 1, :], oc))
                        batches.append(("MULf" if tt == 0 else "MUL", tl))
                    tl = []
                    for r in range(16):
                        qa = QT[pr, 512 * tt + r:512 * tt + 512:16]
                        oc = lambda P, r=r: P[:, r:512:16]
                        tl.append((KT[pr, r:2048:16], qa, 32, V16[:, r, :], oc))
                    batches.append((f"M16_{tt}", tl))
                    for bi, (mk, tl) in enumerate(batches):
                        make_batch(c, tt, s, mk, tl, bi == 0, bi == len(batches) - 1, first_of_pair)
                        first_of_pair = False
            for s in range(2):
                make_sample(c, s)
        pending = []

        def retire(job):
            if job["pre"]:
                job["pre"]()
            job["pv"]()
            if job["fin"]:
                job["fin"]()
            if job.get("last_of_pair") and job["c"] < 7:
                loads_v(job["c"] + 1)

        loads_kq(0)
        loads_v(0)
        for job in jobs:
            if job["ld"] is not None and job["ld"] < 7:
                loads_kq(job["ld"] + 1)
            job["scores"]()
            job["post"]()
            pending.append(job)
            if len(pending) > DEPTH_PIPE:
                retire(pending.pop(0))
        while pending:
            retire(pending.pop(0))

    def ssd_finish(self, l):
        c = self.ssd_carve()
        hT, hTs, acum = c["hT"], c["hTs"], c["acum"]
        o = self.ssd_persist
        def cv(shape, dt):
            nonlocal o
            esz = 2 if dt == BF16 else 4
            nb = (int(np.prod(shape[1:])) * esz + 31) // 32 * 32
            a = self.carve(o, shape, dt)
            o += nb
            return a
        H0 = [cv([128, 1024], F32) for _ in range(2)]
        H0b = [cv([128, 1024], BF16) for _ in range(2)]
        Gb = cv([16, TS], BF16)
        gl = cv([16, 2], F32)
        dg = cv([16, 32], F32)
        gtb = cv([128, 32], F32)
        CTa = cv([128, TS], BF16)
        gsb2 = [cv([128, 512], BF16) for _ in range(3)]
        tmp2 = [cv([128, 512], BF16) for _ in range(3)]
        zt2 = [cv([128, 512], BF16) for _ in range(4)]
        sq = [cv([128, 512], BF16) for _ in range(2)]
        rs = cv([128, 512], F32)
        self.DMA(H0[0], self.xc_out[l][0:128, :], [f"xco{l}"], ["H0p"])
        self.DMA(H0[1], self.sstate_s[l], [], ["H0s"])
        self.TS("dve", H0[0], H0[0], self.FLAG, ALU.mult, ["H0p", "cf32"], ["H0p"])
        self.CP("act", H0b[0], H0[0], ["H0p"], ["H0bp"])
        self.CP("act", H0b[1], H0[1], ["H0s"], ["H0bs"])
        self.A(Gb, acum, AF.Exp, ["acum"], ["Gb"])
        self.A(gl[:, 0:1], acum[:, T - 1:T], AF.Exp, ["acum"], ["gl"])
        self.A(gl[:, 1:2], acum[:, TS - 1:TS], AF.Exp, ["acum"], ["gl"])
        for k in range(2):
            self.TS("dve", dg[:, 16 * k:16 * k + 16], self.IDF[0:16, 0:16], gl[:, k:k + 1], ALU.mult, ["gl", "cf32"], ["dg"])
        self.MM(self.ps[2][:, 0:32], self.ONEF[0:16, :], dg, True, True, ["dg", "cf32"], ["ps2"])
        self.CP("act", gtb, self.ps[2][:, 0:32], ["ps2"], ["gtb"])
        for k, (hl, hk, h0k) in enumerate([(hT, "hT", "H0p"), (hTs, "hTs", "H0s")]):
            h0v = H0[k].rearrange("p (h d) -> p h d", h=16)
            self.TTo("dve", h0v, h0v, gtb[:, 16 * k:16 * k + 16].unsqueeze(2).to_broadcast([128, 16, 64]), ALU.mult, [h0k, "gtb", "H0bp", "H0bs"], [h0k])
            self.TTo("dve", h0v, h0v, hl, ALU.add, [h0k, hk], [h0k])
            self.DMA(self.sstate_out[l, k], H0[k], [h0k], [])
        EEXP = self.cb["EEXP"]
        tiles = [(0, 512, 0), (512, 512, 0), (T, NS, 1)]
        for pr in range(8):
            g = pr // 2
            if pr % 2 == 0:
                self.DMA(CTa, self.xbcT_s[1536 + 128 * g:1536 + 128 * g + 128, :], [f"xbcT{12 + g}"], ["CTa"])
            for (c0, n, k) in tiles:
                it = self._fin_i = getattr(self, "_fin_i", 0) + 1
                pa, pb = [(0, 1), (4, 5), (6, 7)][it % 3]
                gsb, tmp = gsb2[it % 3], tmp2[it % 3]
                gk, tk = f"gsb{it % 3}", f"tmp{it % 3}"
                self.MM(self.ps[pa][:, 0:n], H0b[k][:, 128 * pr:128 * pr + 128], CTa[:, c0:c0 + n], True, True, ["H0bp", "H0bs", "CTa"], [f"ps{pa}"])
                self.MM(self.ps[pb][:, 0:n], EEXP[0:16, 128 * pr:128 * pr + 128], Gb[:, c0:c0 + n], True, True, ["Gb", "cbf"], [f"ps{pb}"])
                self.CP("act", gsb[:, 0:n], self.ps[pb][:, 0:n], [f"ps{pb}"], [gk])
                self.TTo("dve", tmp[:, 0:n], self.ps[pa][:, 0:n], gsb[:, 0:n], ALU.mult, [f"ps{pa}", gk], [tk])
                yv = self.xn[:, 8 + pr, c0:c0 + n]
                self.TTo("pool", yv, yv, tmp[:, 0:n], ALU.add, [tk, f"xn{8 + pr}"], [f"xn{8 + pr}"])
        for (c0, n, k) in tiles:
            for pr in range(8):
                zt, zk = zt2[pr % 4], f"zt{pr % 4}"
                self.DMA(zt[:, 0:n], self.zT_s[128 * pr:128 * pr + 128, c0:c0 + n], [f"zT{pr}"], [zk])
                self.A(zt[:, 0:n], zt[:, 0:n], AF.Silu, [zk], [zk])
                yv = self.xn[:, 8 + pr, c0:c0 + n]
                self.TTo("dve", yv, yv, zt[:, 0:n], ALU.mult, [zk, f"xn{8 + pr}"], [f"xn{8 + pr}"])
                self.A(sq[pr % 2][:, 0:n], yv, AF.Square, [f"xn{8 + pr}"], [f"sq{pr % 2}"])
                self.MM(self.ps[3][:, 0:n], self.cb["ONEB"], sq[pr % 2][:, 0:n], pr == 0, pr == 7, [f"sq{pr % 2}", "cbf"], ["ps3"])
            self.A(rs[:, 0:n], self.ps[3][:, 0:n], AF.Sqrt, ["ps3", "cf32"], ["rs"], bias=self.EPSC, scale=1.0 / 1024)
            self.S.op("dve", lambda e, n=n: e.reciprocal(out=rs[:, 0:n], in_=rs[:, 0:n]), ["rs"], ["rs"])
            for pr in range(8):
                yv = self.xn[:, 8 + pr, c0:c0 + n]
                self.STT(yv, yv, self.pc(l, PC_SNG + pr), rs[:, 0:n], ALU.mult, ALU.mult, [f"xn{8 + pr}", "rs", "pcol"], [f"xn{8 + pr}"])

    def op_phase(self, l, last):
        mix_of = lambda kc: self.xn[:, kc, :]
        mk = lambda kc: f"xn{kc}"
        for st in range(8):
            W, wkey = self.get_w((l, f"o{st}"))
            for j in range(2):
                oc = 2 * st + j
                trip = self.next_trip()
                self.fm_tile(W, wkey, 128 * j, 128, mix_of, KC, trip, mk)
                for ti, (c0, n) in enumerate(CT):
                    xv = self.x[:, oc, c0:c0 + n]
                    self.TTo("dve", xv, xv, self.ps[trip[ti]][:, 0:n], ALU.add, [f"ps{trip[ti]}", f"x{oc}"], [f"x{oc}"])
        if True:
            xk = [f"x{kc}" for kc in range(KC)]
            self.DMA(self.xd_in[l].rearrange("kc (p j) -> p kc j", j=NH), self.x[:, :, T - NH:T], xk, [f"xdi{l}"])
            self.exchange(self.xd_in[l], self.xd_out[l], [f"xdi{l}"], [f"xdo{l}"], f"ccD{l}")
            self.DMA(self.x[:, :, TS:TT], self.xd_out[l][0:KC, :].rearrange("kc (p j) -> p kc j", j=NH), [f"xdo{l}"], xk)

    def ffn_phase(self, l):
        xn_of = lambda kc: self.xn[:, kc, :]
        xk = lambda kc: f"xn{kc}"
        HL = 2 + T + 2 + NS + NH
        NO = HL - 2
        o = 0
        def cv(shape, dt):
            nonlocal o
            esz = 2 if dt == BF16 else 4
            nb = (int(np.prod(shape[1:])) * esz + 31) // 32 * 32
            a_ = self.carve(o, shape, dt)
            o += nb
            return a_
        hp = [cv([128, 1040], F32) for _ in range(2)]
        acc = [cv([128, 1040], F32) for _ in range(2)]
        act = [cv([128, 1040], BF16) for _ in range(8)]
        fst = cv([128, 88, 2], F32)
        fco = cv([128, 88, 4], F32)
        self.DMA(fst, self.fconv_s[l], [], ["fst"])

        def half(f, which, W, wkey, j):
            fi = f + (44 if which else 0)
            trip = self.next_trip()
            self.fm_tile(W, wkey, 128 * j, 128, xn_of, KC, trip, xk)
            H = hp[which]
            hk = [f"hp{which}"]
            self.CP("act", H[:, 2:514], self.ps[trip[0]][:, 0:512], [f"ps{trip[0]}"], hk)
            self.CP("act", H[:, 514:1026], self.ps[trip[1]][:, 0:512], [f"ps{trip[1]}"], hk)
            self.CP("dve", H[:, 1028:1037], self.ps[trip[2]][:, 0:9], [f"ps{trip[2]}"], hk)
            self.CP("pool", H[:, 1026:1028], fst[:, fi, :], ["fst"] + hk, hk)
            self.TS("dve", H[:, 0:2], H[:, 1035:1037], self.FLAG, ALU.mult, hk + ["cf32"], hk)
            self.CP("pool", fco[:, fi, 0:2], H[:, 1024:1026], hk, ["fco"])
            self.CP("pool", fco[:, fi, 2:4], H[:, 1030:1032], hk, ["fco"])
            wcol = lambda t: self.pc(l, PC_FCW + 88 * t + fi)
            ak = [f"acc{which}"]
            self.A(acc[which][:, 0:NO], H[:, 2:2 + NO], AF.Identity, hk + ["pcol"], ak, bias=self.pc(l, PC_FCB + fi), scale=wcol(2))
            for t in (1, 0):
                self.STT(acc[which][:, 0:NO], H[:, t:t + NO], wcol(t), acc[which][:, 0:NO], ALU.mult, ALU.add, hk + ak + ["pcol"], ak)

        def unit(u):
            Wg, wgk = self.get_w((l, f"g{u}"))
            Wu, wuk = self.get_w((l, f"u{u}"))
            for j in range(2):
                f = 2 * u + j
                half(f, 0, Wg, wgk, j)
                half(f, 1, Wu, wuk, j)
                self.A(acc[0][:, 0:NO], acc[0][:, 0:NO], AF.Silu, ["acc0"], ["acc0"])
                self.TTo("pool", act[f % 8][:, 0:NO], acc[0][:, 0:NO], acc[1][:, 0:NO], ALU.mult, ["acc0", "acc1"], [f"act{f % 8}"])

        def down(g):
            Wa, wak = self.get_w((l, f"d{2 * g}"))
            Wb, wbk = self.get_w((l, f"d{2 * g + 1}"))
            cols = [(0, 512), (512, 512), (1026, 9)]
            for oc in range(16):
                trip = self.next_trip()
                for kc in range(4):
                    W, wk = (Wa, wak) if kc < 2 else (Wb, wbk)
                    f = 4 * g + kc
                    for ti, (c0, n) in enumerate(cols):
                        self.MM(self.ps[trip[ti]][:, 0:n], W[:, kc % 2, 128 * oc:128 * oc + 128], act[f % 8][:, c0:c0 + n], kc == 0, kc == 3,
                                [wk, f"act{f % 8}"], [f"ps{trip[ti]}"])
                for ti, (c0, n) in enumerate(CT):
                    xv = self.x[:, oc, c0:c0 + n]
                    self.TTo("dve", xv, xv, self.ps[trip[ti]][:, 0:n], ALU.add, [f"ps{trip[ti]}", f"x{oc}"], [f"x{oc}"])

        for u in range(22):
            unit(u)
            if u >= 2 and u % 2 == 0:
                down(u // 2 - 1)
        down(10)
        self.DMA(self.fconv_out[l], fco, ["fco"], [])

    def build(self):
        cfg = self.cfg
        self.plan_weights()
        self.load_consts()
        self.S.tag = "scr"
        for l in range(self.L):
            if cfg.stop_after == "load":
                break
            last = l == self.L - 1
            self.S.tag = None
            self.DMA(self.pcol, self.pcol_d[:, l * NPC:(l + 1) * NPC], [], ["pcol"])
            self.barrier()
            self.norm_phase(l, PC_G1, 2048.0)
            if cfg.stop_after == "n1":
                self.S.tag = None
                self.DMA(self.mix_dbg, self.xn, [f"xn{kc}" for kc in range(KC)], [])
                break
            self.barrier()
            self.ip_phase(l)
            if cfg.stop_after == "ip":
                break
            self.barrier()
            self.ssd_local(l)
            self.barrier()
            self.att_phase(l)
            self.barrier()
            self.ssd_finish(l)
            if cfg.dbg and l == self.L - 1:
                self.DMA(self.mix_dbg, self.xn, [f"xn{kc}" for kc in range(KC)], [])
            if cfg.stop_after == "mix":
                break
            self.barrier()
            self.op_phase(l, last and cfg.stop_after is None)
            if cfg.stop_after == "op":
                break
            self.barrier()
            self.norm_phase(l, PC_G2, 2048.0)
            self.barrier()
            self.ffn_phase(l)
        self.S.tag = None
        yv = self.yT.rearrange("(kc p) t -> p kc t", p=128)
        for kc in range(KC):
            self.DMA(yv[:, kc, :], self.x[:, kc, :], [f"x{kc}"], [])
        self.S.emit()
        return self.nc


def _consts(hf):
    flag = np.float32(hf)
    lay, ncb = Builder.cbf_layout()
    cbf = np.zeros((128, ncb), np.float32)
    p = np.arange(128)[:, None]
    f = np.arange(128)[None, :]
    Uatt = (p >= f).astype(np.float32)
    Latt = (p <= f).astype(np.float32)
    def put(n, a):
        o, w = lay[n]
        cbf[:a.shape[0], o:o + w] = a
    put("MUL", np.concatenate([Uatt, Latt, Uatt, Latt], 1))
    put("MULf", np.concatenate([flag * Uatt, Latt, flag * Uatt, Latt], 1))
    put("MULf1", np.concatenate([flag * Uatt, Latt, Uatt, Latt], 1))
    for tt in range(2):
        m = np.zeros((128, 16, 32), np.float32)
        m[:64] = flag
        q = np.arange(32)[None, :]
        pp = np.arange(64)[:, None]
        m[64:] = (pp <= 32 * tt + q).astype(np.float32)[:, None, :]
        put(f"M16_{tt}", m.reshape(128, 512))
    put("IDB", np.eye(128, dtype=np.float32))
    put("ONEB", np.ones((128, 128), np.float32))
    bd = np.zeros((128, 128), np.float32)
    bd[:64, :64] = 1.0 / 64
    bd[64:, 64:] = 1.0 / 64
    put("BD64", bd)
    ee = np.zeros((16, 1024), np.float32)
    for h in range(16):
        ee[h, 64 * h:64 * h + 64] = 1.0
    put("EEXP", ee)
    sm = np.zeros((128, 40), np.float32)
    for l_ in range(4):
        sm[:, l_] = (np.arange(128) >= l_)
        for st in range(1, 9):
            if (st - 1) % 4 == l_:
                sm[:, 4 * st + l_] = 1.0
        for kj in range(4):
            sm[kj, 36 + l_] = 3.0 if kj == l_ else (1.0 if kj < l_ else 0.0)
    put("SM", sm)
    Rb = np.zeros((128, 128), np.float32)
    for m_ in range(128):
        if m_ % 64 < 32:
            Rb[m_ + 32, m_] = -1.0
        else:
            Rb[m_ - 32, m_] = 1.0
    put("RROTB", Rb)
    cf = np.zeros((128, 4 * 128 + 4), np.float32)
    cf[:, 0:128] = np.eye(128)
    cf[:, 128:256] = 1.0
    R = np.zeros((128, 128), np.float32)
    for m_ in range(128):
        if m_ % 64 < 32:
            R[m_ + 32, m_] = -1.0
        else:
            R[m_ - 32, m_] = 1.0
    cf[:, 256:384] = R
    cf[:, 384:512] = (p <= f)
    cf[:, 512] = flag
    cf[:, 513] = EPS
    half = 32
    inv = (np.float32(10000.0) ** (-np.arange(half, dtype=np.float32) / np.float32(half))).astype(np.float32)
    pos = np.concatenate([hf * T + np.arange(T), PAST + np.arange(NS), np.maximum(hf * T - NH + np.arange(NH), 0)]).astype(np.float32)
    ang = pos[None, :] * inv[:, None]
    cs = np.zeros((128, 2, TT), np.float32)
    cs[:, 0, :] = np.tile(np.cos(ang).astype(np.float32), (4, 1))
    sn = np.sin(ang).astype(np.float32)
    cs[:, 1, :] = np.concatenate([-sn, sn, -sn, sn], 0)
    return cbf.astype(bf16_np), cf, cs


def _pcol(inp):
    pc = np.zeros((128, DEPTH * NPC), np.float32)
    for l in range(DEPTH):
        b = l * NPC
        pc[:, b + PC_G1:b + PC_G1 + 16] = inp["norm1_g"][l].reshape(16, 128).T
        pc[:, b + PC_G2:b + PC_G2 + 16] = inp["norm2_g"][l].reshape(16, 128).T
        for t in range(4):
            pc[:, b + PC_SCW + 16 * t:b + PC_SCW + 16 * t + 16] = inp["ssd_conv_w"][l, t].reshape(16, 128).T
        pc[:, b + PC_SCB:b + PC_SCB + 16] = inp["ssd_conv_b"][l].reshape(16, 128).T
        for t in range(3):
            pc[:, b + PC_FCW + 88 * t:b + PC_FCW + 88 * t + 88] = inp["ffn_conv_w"][l, t].reshape(88, 128).T
        pc[:, b + PC_FCB:b + PC_FCB + 88] = inp["ffn_conv_b"][l].reshape(88, 128).T
        pc[:, b + PC_QG] = np.tile(inp["q_norm_g"][l], 2)
        pc[:, b + PC_KG] = np.tile(inp["k_norm_g"][l], 2)
        pc[:, b + PC_SNG:b + PC_SNG + 8] = inp["ssd_norm_g"][l].reshape(8, 128).T
        pc[:, b + PC_DCOL:b + PC_DCOL + 8] = np.repeat(inp["ssd_d"][l], 64).reshape(8, 128).T
        pc[0:16, b + PC_DTB] = inp["ssd_dt_bias"][l]
        pc[0:16, b + PC_ALOG] = inp["ssd_a_log"][l]
    return pc


_SEL_ROWS = np.concatenate([np.arange(1920, 2048)] + [1536 + l_ + 4 * np.arange(128) for l_ in range(4)]
                           + [l_ + 16 * np.arange(128) for l_ in range(4)])


def make_in_maps(inp, nl=DEPTH):
    f32 = np.float32
    xp, xs = inp["x_prompt"], inp["x_sample"]
    pcol = _pcol(inp)
    shared = {k: np.ascontiguousarray(inp[k][:nl], dtype=f32) for k in ("w_in", "w_out", "w_up", "w_down")}
    maps = []
    for c in range(8):
        b, hf = c // 2, c % 2
        cbf, cf, cs = _consts(hf)
        own = xp[b, hf * T:(hf + 1) * T]
        halo = xp[b, T - NH:T] if hf == 1 else np.zeros((NH, D), f32)
        xT0 = np.ascontiguousarray(np.concatenate([own, xs[c], halo], 0).T, dtype=f32)
        ck = inp["cache_win_k"][:, c][:, _SEL_ROWS]
        cv = inp["cache_win_v"][:, c][:, _SEL_ROWS]
        ksel = np.ascontiguousarray(ck.reshape(DEPTH, 1152, 1024).transpose(0, 2, 1), dtype=f32)
        vsel = np.ascontiguousarray(cv.reshape(DEPTH, 1152, 1024), dtype=f32)
        sconv_s = np.ascontiguousarray(inp["state_ssd_conv"][:, c].transpose(0, 2, 1).reshape(DEPTH, 16, 128, 3).transpose(0, 2, 1, 3), dtype=f32)
        sstate_s = np.ascontiguousarray(inp["state_ssd"][:, c].transpose(0, 3, 1, 2).reshape(DEPTH, 128, 1024), dtype=f32)
        fconv_s = np.ascontiguousarray(inp["state_ffn_conv"][:, c].transpose(0, 2, 1).reshape(DEPTH, 88, 128, 2).transpose(0, 2, 1, 3), dtype=f32)
        m = dict(shared)
        m.update(xT0=xT0, pcol=pcol, cossin=cs, cbf=cbf, cf32=cf, ksel=ksel[:nl], vsel=vsel[:nl], sconv_s=sconv_s[:nl],
                 sstate_s=sstate_s[:nl], fconv_s=fconv_s[:nl])
        maps.append(m)
    return maps


def assemble(results):
    f32 = np.float32
    L = DEPTH
    yp = np.zeros((4, 2048, D), f32)
    ys = np.zeros((8, NS, D), f32)
    wkp = np.zeros((L, 4, 2048, 16, 64), f32)
    wvp = np.zeros((L, 4, 2048, 16, 64), f32)
    wks = np.zeros((L, 8, NS, 16, 64), f32)
    wvs = np.zeros((L, 8, NS, 16, 64), f32)
    scp = np.zeros((L, 4, 3, 2048), f32)
    scs = np.zeros((L, 8, 3, 2048), f32)
    ssp = np.zeros((L, 4, 16, 64, 128), f32)
    sss = np.zeros((L, 8, 16, 64, 128), f32)
    fcp = np.zeros((L, 4, 2, 2 * D_FF), f32)
    fcs = np.zeros((L, 8, 2, 2 * D_FF), f32)
    for c in range(8):
        r = {k: np.asarray(v) for k, v in results[c].items()}
        b, hf = c // 2, c % 2
        yT = r["yT"]
        yp[b, hf * T:(hf + 1) * T] = yT[:, 0:T].T
        ys[c] = yT[:, T:TS].T
        kT = r["kT_out"]
        wkp[:, b, hf * T:(hf + 1) * T] = kT[:, :, 0:T].transpose(0, 2, 1).reshape(L, T, 16, 64)
        wks[:, c] = kT[:, :, T:TS].transpose(0, 2, 1).reshape(L, NS, 16, 64)
        v = r["v_out"]
        wvp[:, b, hf * T:(hf + 1) * T] = v[:, 0:T].reshape(L, T, 16, 64)
        wvs[:, c] = v[:, T:TS].reshape(L, NS, 16, 64)
        sc = r["sconv_out"]
        sc = sc.transpose(0, 2, 1, 3).reshape(L, 2048, 6)
        if hf == 1:
            scp[:, b] = sc[:, :, 0:3].transpose(0, 2, 1)
        scs[:, c] = sc[:, :, 3:6].transpose(0, 2, 1)
        st = r["sstate_out"]
        st = st.reshape(L, 2, 128, 16, 64).transpose(0, 1, 3, 4, 2)
        if hf == 1:
            ssp[:, b] = st[:, 0]
        sss[:, c] = st[:, 1]
        fc = r["fconv_out"]
        fc = fc.transpose(0, 2, 1, 3).reshape(L, 2 * D_FF, 4)
        if hf == 1:
            fcp[:, b] = fc[:, :, 0:2].transpose(0, 2, 1)
        fcs[:, c] = fc[:, :, 2:4].transpose(0, 2, 1)
    return (yp, ys, wkp, wvp, wks, wvs, scp, scs, ssp, sss, fcp, fcs)


_NC_CACHE = {}


def kernel(**inputs):
    inp = {k: np.asarray(v) for k, v in inputs.items()}
    if "nc" not in _NC_CACHE:
        _NC_CACHE["nc"] = Builder(Cfg()).build()
    nc = _NC_CACHE["nc"]
    maps = make_in_maps(inp)
    res = run_bass_kernel_spmd(nc, maps, core_ids=list(range(8)))
    return assemble(res.results)
```

```python
import contextlib
import numpy as np
import ml_dtypes
import concourse.bass as bass
import concourse.mybir as mybir
from concourse.bass_utils import run_bass_kernel_spmd

F32 = mybir.dt.float32
BF16 = mybir.dt.bfloat16
ALU = mybir.AluOpType
AF = mybir.ActivationFunctionType
bf16_np = ml_dtypes.bfloat16

DEPTH = 4
D = 2048
KC = 16
T = 1024
NS = 4
NH = 5
TT = T + NS + NH
TS = T + NS
CT = [(0, 345), (345, 344), (689, 344)]
D_IN = 6160
D_FF = 5632
NPC = 484
PAST = 16384
EPS = 1e-6
PC_G1, PC_G2, PC_SCW, PC_SCB, PC_FCW, PC_FCB, PC_QG, PC_KG, PC_SNG, PC_DCOL, PC_DTB, PC_ALOG = \
    0, 16, 32, 96, 112, 376, 464, 465, 466, 474, 482, 483


class Op:
    __slots__ = ("eng", "fn", "reads", "writes", "dma", "stream", "idx", "deps", "need_inc", "count", "cc")

    def __init__(self, eng, fn, reads, writes, dma=False, stream=None, cc=False):
        self.eng, self.fn, self.reads, self.writes = eng, fn, reads, writes
        self.dma, self.stream, self.cc = dma, stream, cc
        self.deps, self.need_inc, self.count = [], False, None


class Sched:
    def __init__(self, nc):
        self.nc = nc
        self.ops = []
        self.last_writer = {}
        self.readers = {}
        self.tag = None

    def _add(self, op):
        op.idx = len(self.ops)
        if self.tag is not None:
            op.reads = tuple(op.reads) + (self.tag,)
        deps = set()
        for r in op.reads:
            w = self.last_writer.get(r)
            if w is not None:
                deps.add(w)
            if r.startswith("ps"):
                for k_, i_ in self.readers.get(r, {}).items():
                    if k_ != op.eng:
                        deps.add(i_)
        for w_ in op.writes:
            w = self.last_writer.get(w_)
            if w is not None:
                deps.add(w)
            deps.update(self.readers.get(w_, {}).values())
        deps.discard(op.idx)
        op.deps = sorted(deps)
        rk = ("D", op.idx) if op.dma else op.eng
        for r in op.reads:
            self.readers.setdefault(r, {})[rk] = op.idx
        for w_ in op.writes:
            self.last_writer[w_] = op.idx
            self.readers[w_] = {}
        self.ops.append(op)
        return op

    def op(self, eng, fn, reads=(), writes=()):
        return self._add(Op(eng, fn, tuple(reads), tuple(writes)))

    def dma(self, queue, out, in_, reads=(), writes=(), stream=None):
        def fn(e, out=out, in_=in_):
            return e.dma_start(out=out, in_=in_)
        return self._add(Op(queue, fn, tuple(reads), tuple(writes), dma=True, stream=stream))

    def collective(self, fn, reads=(), writes=(), stream=None):
        return self._add(Op("pool", fn, tuple(reads), tuple(writes), dma=True, stream=stream, cc=True))

    def _skip(self, do, o):
        return (not do.dma) and (not o.dma) and do.eng == o.eng and do.eng == "pe"

    def emit(self, final_wait_engine="sp"):
        nc, ops = self.nc, self.ops
        for o in ops:
            if o.dma:
                o.need_inc = True
            for d in o.deps:
                do = ops[d]
                if not self._skip(do, o):
                    do.need_inc = True
        eng_cnt, stream_cnt = {}, {}
        for o in ops:
            if not o.need_inc:
                continue
            if o.dma:
                stream_cnt[o.stream] = stream_cnt.get(o.stream, 0) + (1 if o.cc else 16)
                o.count = stream_cnt[o.stream]
            else:
                eng_cnt[o.eng] = eng_cnt.get(o.eng, 0) + 1
                o.count = eng_cnt[o.eng]
        streams, engs = sorted(stream_cnt), sorted(eng_cnt)
        self.stats = dict(eng_cnt=eng_cnt, n_streams=len(streams), n_ops=len(ops))
        with contextlib.ExitStack() as es:
            sems = {}
            for e in engs:
                sems[("E", e)] = es.enter_context(nc.semaphore("p_" + e))
            for s in streams:
                sems[("S", s)] = es.enter_context(nc.semaphore("d_" + str(s)))
            block = es.enter_context(nc.Block())
            per_eng = {}
            for o in ops:
                per_eng.setdefault(o.eng, []).append(o)
            per_eng.setdefault(final_wait_engine, [])

            def run(eng_name, e):
                known = {}
                for o in per_eng[eng_name]:
                    need = {}
                    for d in o.deps:
                        do = ops[d]
                        if (not do.need_inc) or self._skip(do, o):
                            continue
                        k = ("S", do.stream) if do.dma else ("E", do.eng)
                        if do.count > need.get(k, 0):
                            need[k] = do.count
                    for k, v in need.items():
                        if known.get(k, 0) >= v:
                            continue
                        e.wait_ge(sems[k], v)
                        known[k] = v
                    ins = o.fn(e)
                    if o.need_inc:
                        if o.dma:
                            ins.then_inc(sems[("S", o.stream)], 1 if o.cc else 16)
                        else:
                            ins.then_inc(sems[("E", o.eng)], 1)
                if eng_name == final_wait_engine:
                    for s in streams:
                        e.wait_ge(sems[("S", s)], stream_cnt[s])
                    for en in engs:
                        e.wait_ge(sems[("E", en)], eng_cnt[en])

            if "pe" in per_eng:
                @block.tensor
                def _(e):
                    run("pe", e)
            if "act" in per_eng:
                @block.scalar
                def _(e):
                    run("act", e)
            if "dve" in per_eng:
                @block.vector
                def _(e):
                    run("dve", e)
            if "pool" in per_eng:
                @block.gpsimd
                def _(e):
                    run("pool", e)
            if "sp" in per_eng:
                @block.sync
                def _(e):
                    run("sp", e)


class Cfg:
    def __init__(self, n_layers=DEPTH, stop_after=None, dbg=False, no_cc=False):
        self.no_cc = no_cc
        self.n_layers = n_layers
        self.stop_after = stop_after
        self.dbg = dbg


class Builder:
    def __init__(self, cfg):
        self.cfg = cfg
        self.nc = nc = bass.Bass("TRN2", target_bir_lowering=False)
        self.S = Sched(nc)
        self.L = cfg.n_layers
        self.uid = 0
        self.decl_dram()
        self.decl_sbuf()

    def din(self, name, shape, dt=F32):
        return self.nc.dram_tensor(name, list(shape), dt, kind="ExternalInput").ap()

    def dout(self, name, shape, dt=F32):
        return self.nc.dram_tensor(name, list(shape), dt, kind="ExternalOutput").ap()

    def dint(self, name, shape, dt, dbg=False):
        kind = "ExternalOutput" if (dbg and self.cfg.dbg) else "Internal"
        return self.nc.dram_tensor(name, list(shape), dt, kind=kind).ap()

    def decl_dram(self):
        L = self.cfg.n_layers
        self.xT0 = self.din("xT0", [D, TT])
        self.w_in = self.din("w_in", [L, D, D_IN])
        self.w_out = self.din("w_out", [L, D, D])
        self.w_up = self.din("w_up", [L, D, 2 * D_FF])
        self.w_down = self.din("w_down", [L, D_FF, D])
        self.pcol_d = self.din("pcol", [128, DEPTH * NPC])
        self.cossin_d = self.din("cossin", [128, 2, TT])
        self.cbf_d = self.din("cbf", [128, self.cbf_layout()[1]], BF16)
        self.cf32_d = self.din("cf32", [128, 4 * 128 + 4])
        self.ksel = self.din("ksel", [L, 1024, 1152])
        self.vsel = self.din("vsel", [L, 1152, 1024])
        self.sconv_s = self.din("sconv_s", [L, 128, 16, 3])
        self.sstate_s = self.din("sstate_s", [L, 128, 1024])
        self.fconv_s = self.din("fconv_s", [L, 128, 88, 2])
        self.yT = self.dout("yT", [D, TT])
        self.kT_out = self.dout("kT_out", [L, 1024, TT])
        self.v_out = self.dout("v_out", [L, TT, 1024])
        self.sconv_out = self.dout("sconv_out", [L, 128, 16, 6])
        self.sstate_out = self.dout("sstate_out", [L, 2, 128, 1024])
        self.fconv_out = self.dout("fconv_out", [L, 128, 88, 4])
        self.xk_in = [self.dint(f"xk_in{l}", [1024, 1024], BF16) for l in range(L)]
        self.xk_out = [self.dint(f"xk_out{l}", [2048, 1024], BF16) for l in range(L)]
        self.xv_in = [self.dint(f"xv_in{l}", [1024, 1024], BF16) for l in range(L)]
        self.xv_out = [self.dint(f"xv_out{l}", [2048, 1024], BF16) for l in range(L)]
        self.xc_in = [self.dint(f"xc_in{l}", [128, 1024], F32) for l in range(L)]
        self.xc_out = [self.dint(f"xc_out{l}", [256, 1024], F32) for l in range(L)]
        self.xd_in = [self.dint(f"xd_in{l}", [KC, 128 * NH], F32) for l in range(L)]
        self.xd_out = [self.dint(f"xd_out{l}", [2 * KC, 128 * NH], F32) for l in range(L)]
        self.qT_s = self.dint("qT_s", [1024, TS], BF16, dbg=True)
        self.xbcT_s = self.dint("xbcT_s", [2048, TS], BF16, dbg=True)
        self.zT_s = self.dint("zT_s", [1024, TS], BF16, dbg=True)
        if self.cfg.dbg:
            self.mix_dbg = self.dout("mix_dbg", [128, 16, TT], BF16)
            self.x_dbg = self.dout("x_dbg", [D, TT])

    @staticmethod
    def cbf_layout():
        names = [("MUL", 512), ("MULf", 512), ("MULf1", 512), ("M16_0", 512), ("M16_1", 512),
                 ("IDB", 128), ("ONEB", 128), ("BD64", 128), ("EEXP", 1024), ("SM", 40), ("ZB", 128), ("RROTB", 128)]
        off, lay = 0, {}
        for n, w in names:
            lay[n] = (off, w)
            off += w
        return lay, off

    def sb(self, name, shape, dt):
        return self.nc.alloc_sbuf_tensor(name, list(shape), dt).ap()

    def decl_sbuf(self):
        nc = self.nc
        self.x = self.sb("x", [128, KC, TT], F32)
        self.xn = self.sb("xn", [128, KC, TT], BF16)
        self.wst = self.sb("wst", [128, 16384], BF16)
        self.pcol = self.sb("pcol_sb", [128, NPC], F32)
        self.cossin = self.sb("cossin_sb", [128, 2, TT], F32)
        lay, ncb = self.cbf_layout()
        self.cbf = self.sb("cbf_sb", [128, ncb], BF16)
        self.cb = {n: self.cbf[:, o:o + w] for n, (o, w) in lay.items()}
        self.cf32 = self.sb("cf32_sb", [128, 4 * 128 + 4], F32)
        self.IDF = self.cf32[:, 0:128]
        self.ONEF = self.cf32[:, 128:256]
        self.RROT = self.cf32[:, 256:384]
        self.UTRI = self.cf32[:, 384:512]
        self.FLAG = self.cf32[:, 512:513]
        self.EPSC = self.cf32[:, 513:514]
        self.dta = self.sb("dta", [16, 2, TS], F32)
        self.ksn = self.sb("ksn", [128, 8, NS], BF16)
        self.vsn = self.sb("vsn", [NS, 1024], BF16)
        self.NSCR = 48800
        self.scr = self.sb("scr", [128, self.NSCR // 4], F32)
        self.ps = [nc.alloc_psum_tensor(f"ps{i}", [128, 512], F32).ap() for i in range(8)]
        self.trip_i = 0

    def carve(self, off, shape, dt):
        esz = 2 if dt == BF16 else 4
        n = int(np.prod(shape[1:]))
        assert off % 4 == 0 and off + n * esz <= self.NSCR, (off, shape)
        nf = (n * esz + 3) // 4
        a = self.scr[0:shape[0], off // 4: off // 4 + nf]
        if dt == BF16:
            a = a.bitcast(BF16)[:, 0:n]
        if len(shape) == 2:
            return a
        names = " ".join(f"d{i}" for i in range(1, len(shape)))
        kw = {f"d{i}": shape[i] for i in range(1, len(shape))}
        return a.rearrange(f"p ({names}) -> p {names}", **kw)

    @staticmethod
    def segs(c0, n):
        out = []
        if c0 < T:
            out.append((c0, min(c0 + n, T)))
        if c0 + n > T:
            out.append((max(c0, T), c0 + n))
        return out

    def pc(self, l, off, n=1):
        return self.pcol[:, off: off + n]

    def next_trip(self):
        t = (0, 1, 2) if self.trip_i % 2 == 0 else (3, 4, 5)
        self.trip_i += 1
        return t

    def A(self, out, in_, func, reads, writes, bias=None, scale=None):
        kw = {}
        if bias is not None:
            kw["bias"] = bias
        if scale is not None:
            kw["scale"] = scale
        self.S.op("act", lambda e: e.activation(out=out, in_=in_, func=func, **kw), reads, writes)

    def TTo(self, eng, out, in0, in1, op, reads, writes):
        self.S.op(eng, lambda e: e.tensor_tensor(out=out, in0=in0, in1=in1, op=op), reads, writes)

    def STT(self, out, in0, scalar, in1, op0, op1, reads, writes):
        self.S.op("dve", lambda e: e.scalar_tensor_tensor(out=out, in0=in0, scalar=scalar, in1=in1, op0=op0, op1=op1),
                  reads, writes)

    def TS(self, eng, out, in0, s1, op0, reads, writes, s2=None, op1=None):
        if op1 is None:
            self.S.op(eng, lambda e: e.tensor_scalar(out=out, in0=in0, scalar1=s1, scalar2=None, op0=op0), reads, writes)
        else:
            self.S.op(eng, lambda e: e.tensor_scalar(out=out, in0=in0, scalar1=s1, scalar2=s2, op0=op0, op1=op1), reads, writes)

    def CP(self, eng, out, in_, reads, writes):
        if eng == "act":
            self.S.op("act", lambda e: e.activation(out=out, in_=in_, func=AF.Copy), reads, writes)
        else:
            self.S.op(eng, lambda e: e.tensor_copy(out=out, in_=in_), reads, writes)

    def MM(self, out, lhsT, rhs, start, stop, reads, writes):
        self.S.op("pe", lambda e: e.matmul(out, lhsT=lhsT, rhs=rhs, start=start, stop=stop), reads, writes)

    def TR(self, out, in_, ident, reads, writes):
        self.S.op("pe", lambda e: e.transpose(out, in_, ident), reads, writes)

    def DMA(self, out, in_, reads, writes, q="sp", stream=None):
        if stream is None:
            self.uid += 1
            stream = f"u{self.uid % 24}"
        self.S.dma(q, out, in_, reads, writes, stream)

    def barrier(self):
        d = self.cf32[0:1, 515:516]
        self.S.tag = None
        self.S.op("act", lambda e: e.activation(out=d, in_=self.cf32[0:1, 514:515], func=AF.Copy), reads=["bar_d"], writes=["scr", "bar_d"])
        self.S.tag = "scr"

    def plan_weights(self):
        st = []
        for l in range(self.L):
            wi = self.w_in[l].rearrange("(kc p) n -> p kc n", p=128)
            for name, c0 in [("k", 1024), ("v", 2048), ("q", 0), ("z", 3072)]:
                for i in range(4):
                    st.append(((l, f"{name}{i}"), wi[:, :, c0 + 256 * i:c0 + 256 * i + 256], (KC, 256)))
            for i in range(8):
                st.append(((l, f"x{i}"), wi[:, :, 4096 + 256 * i:4096 + 256 * i + 256], (KC, 256)))
            st.append(((l, "dt"), wi[:, :, 6144:6160], (KC, 16)))
            wo = self.w_out[l].rearrange("(kc p) n -> p kc n", p=128)
            for i in range(8):
                st.append(((l, f"o{i}"), wo[:, :, 256 * i:256 * i + 256], (KC, 256)))
            wu = self.w_up[l].rearrange("(kc p) n -> p kc n", p=128)

            def unit(u):
                st.append(((l, f"g{u}"), wu[:, :, 256 * u:256 * u + 256], (KC, 256)))
                st.append(((l, f"u{u}"), wu[:, :, D_FF + 256 * u:D_FF + 256 * u + 256], (KC, 256)))

            def down(g):
                for h in range(2):
                    r0 = 512 * g + 256 * h
                    st.append(((l, f"d{2 * g + h}"), self.w_down[l][r0:r0 + 256, :].rearrange("(kc p) n -> p kc n", p=128), (2, 2048)))
            for u in range(22):
                unit(u)
                if u >= 2 and u % 2 == 0:
                    down(u // 2 - 1)
            down(10)
        self.wplan = st
        self.wpos = {k: i for i, (k, _, _) in enumerate(st)}
        self.wissued = 0
        self.wnext = 0

    def wview(self, slot, shp):
        a, b = shp
        return self.wst[:, 4096 * slot:4096 * slot + a * b].rearrange("p (a b) -> p a b", a=a)

    def issue_w(self, upto):
        tag, self.S.tag = self.S.tag, None
        while self.wissued <= min(upto, len(self.wplan) - 1):
            i = self.wissued
            key, src, shp = self.wplan[i]
            slot = i % 4
            self.S.dma("pool", self.wview(slot, shp), src, reads=[], writes=[f"wst{slot}"], stream=f"w{slot}")
            self.wissued += 1
        self.S.tag = tag

    def get_w(self, key):
        i = self.wpos[key]
        assert i == self.wnext, (key, i, self.wnext)
        self.wnext += 1
        self.issue_w(i + 2)
        _, _, shp = self.wplan[i]
        return self.wview(i % 4, shp), f"wst{i % 4}"

    def fm_tile(self, W, wkey, j0, M, rhs_of, nk, trip, rkeys):
        for kc in range(nk):
            for ti, (c0, n) in enumerate(CT):
                b = trip[ti]
                self.MM(self.ps[b][0:M, 0:n], W[:, kc, j0:j0 + M], rhs_of(kc)[:, c0:c0 + n], kc == 0, kc == nk - 1,
                        reads=[wkey, rkeys(kc)], writes=[f"ps{b}"])

    def load_consts(self):
        S = self.S
        self.DMA(self.cossin, self.cossin_d, [], ["cossin"])
        self.DMA(self.cbf, self.cbf_d, [], ["cbf"])
        self.DMA(self.cf32, self.cf32_d, [], ["cf32", "bar_d"])
        xv = self.xT0.rearrange("(kc p) t -> p kc t", p=128)
        for kc in range(KC):
            self.DMA(self.x[:, kc, :], xv[:, kc, :], [], [f"x{kc}"])
        S.op("pool", lambda e: e.memset(self.xn[:, :, TS:TT], 0.0), [], [f"xn{kc}" for kc in range(KC)])

    def norm_phase(self, l, goff, div):
        import os
        lvl = int(os.environ.get("N1LVL", "9"))
        sqb = [self.carve(0, [128, TT], BF16), self.carve(2080, [128, TT], BF16)]
        rstd = self.carve(4160, [128, TT], F32)
        trip = self.next_trip()
        for kc in range(KC):
            sq = sqb[kc % 2]
            if lvl >= 1:
                if kc % 2 == 0:
                    self.A(sq, self.x[:, kc, :], AF.Square, [f"x{kc}"], [f"sqb{kc % 2}"])
                else:
                    self.TTo("dve", sq, self.x[:, kc, :], self.x[:, kc, :], ALU.mult, [f"x{kc}"], [f"sqb{kc % 2}"])
            if lvl >= 2:
                for ti, (c0, n) in enumerate(CT):
                    self.MM(self.ps[trip[ti]][:, 0:n], self.cb["ONEB"], sq[:, c0:c0 + n], kc == 0, kc == KC - 1,
                            [f"sqb{kc % 2}", "cbf"], [f"ps{trip[ti]}"])
        if lvl >= 3:
            for ti, (c0, n) in enumerate(CT):
                self.A(rstd[:, c0:c0 + n], self.ps[trip[ti]][:, 0:n], AF.Sqrt, [f"ps{trip[ti]}", "cf32"], ["rstd"],
                       bias=self.EPSC, scale=1.0 / div)
        if lvl >= 4:
            self.S.op("dve", lambda e: e.reciprocal(out=rstd, in_=rstd), ["rstd"], ["rstd"])
        if lvl >= 5:
            for kc in range(KC):
                self.STT(self.xn[:, kc, :], self.x[:, kc, :], self.pc(l, goff + kc), rstd, ALU.mult, ALU.mult,
                         [f"x{kc}", "rstd", "pcol"], [f"xn{kc}"])

    def ip_phase(self, l):
        xn_of = lambda kc: self.xn[:, kc, :]
        xk = lambda kc: f"xn{kc}"
        o = 0
        def cv(shape, dt):
            nonlocal o
            esz = 2 if dt == BF16 else 4
            nb = (int(np.prod(shape[1:])) * esz + 31) // 32 * 32
            a_ = self.carve(o, shape, dt)
            o += nb
            return a_
        sqb = [cv([128, 1040], BF16) for _ in range(2)]
        t1 = [cv([128, 1040], F32) for _ in range(2)]
        t2 = [cv([128, 1040], F32) for _ in range(2)]
        t3 = [cv([128, 1040], F32) for _ in range(2)]
        kb = [cv([128, 1040], BF16) for _ in range(2)]
        t2b = [cv([128, 1040], BF16) for _ in range(2)]
        xgb = [cv([128, 1040], BF16) for _ in range(2)]
        vf = [cv([128, 256], F32) for _ in range(2)]
        vb = [cv([128, 256], BF16) for _ in range(2)]
        sst = cv([128, 16, 3], F32)
        sco = cv([128, 16, 6], F32)
        acol = cv([16, 1], F32)
        cos, sin = self.cossin[:, 0, :], self.cossin[:, 1, :]

        def qk_stage0(item):
            kind, c, slot = item[0], item[1], item[2]
            st, j = c // 2, c % 2
            if j == 0:
                self._qkW = self.get_w((l, f"{kind}{st}"))
            W, wkey = self._qkW
            trip = self.next_trip()
            item.append(trip)
            self.fm_tile(W, wkey, 128 * j, 128, xn_of, KC, trip, xk)

        def qk_stage0b(item):
            kind, c, slot, trip = item
            gcol = self.pc(l, PC_KG if kind == "k" else PC_QG)
            for ti, (c0, n) in enumerate(CT):
                self.A(sqb[slot][:, c0:c0 + n], self.ps[trip[ti]][:, 0:n], AF.Square, [f"ps{trip[ti]}"], [f"sqb{slot}"])
                self.TS("dve", xgb[slot][:, c0:c0 + n], self.ps[trip[ti]][:, 0:n], gcol, ALU.mult, [f"ps{trip[ti]}", "pcol"], [f"xgb{slot}"])

        def qk_stage1(item):
            kind, c, slot, trip = item
            for ti, (c0, n) in enumerate(CT):
                mb = 6 + (ti % 2)
                self.MM(self.ps[mb][:, 0:n], self.cb["BD64"], sqb[slot][:, c0:c0 + n], True, True, [f"sqb{slot}", "cbf"], [f"ps{mb}"])
                self.A(t1[slot][:, c0:c0 + n], self.ps[mb][:, 0:n], AF.Sqrt, [f"ps{mb}", "cf32"], [f"t1{slot}"], bias=self.EPSC, scale=1.0)
            self.S.op("dve", lambda e: e.reciprocal(out=t1[slot][:, 0:TT], in_=t1[slot][:, 0:TT]), [f"t1{slot}"], [f"t1{slot}"])
            self.TTo("pool", t2b[slot][:, 0:TT], xgb[slot][:, 0:TT], t1[slot][:, 0:TT], ALU.mult, [f"xgb{slot}", f"t1{slot}"], [f"t2b{slot}"])

        def qk_stage2(item):
            kind, c, slot, trip = item
            self.TTo("pool", t3[slot][:, 0:TT], t2b[slot][:, 0:TT], cos, ALU.mult, [f"t2b{slot}", "cossin"], [f"t3{slot}"])
            for ti, (c0, n) in enumerate(CT):
                mb = 6 + (ti % 2)
                self.MM(self.ps[mb][:, 0:n], self.cb["RROTB"], t2b[slot][:, c0:c0 + n], True, True, [f"t2b{slot}", "cbf"], [f"ps{mb}"])
                self.TTo("dve", t1[slot][:, c0:c0 + n], self.ps[mb][:, 0:n], sin[:, c0:c0 + n], ALU.mult, [f"ps{mb}", "cossin"], [f"t1{slot}"])
            if kind == "k":
                self.TTo("pool", t3[slot][:, 0:TT], t3[slot][:, 0:TT], t1[slot][:, 0:TT], ALU.add, [f"t3{slot}", f"t1{slot}"], [f"t3{slot}"])
                self.DMA(self.kT_out[l, 128 * c:128 * c + 128, :], t3[slot][:, 0:TT], [f"t3{slot}"], [])
                self.CP("act", kb[slot][:, 0:TS], t3[slot][:, 0:TS], [f"t3{slot}"], [f"kb{slot}"])
                self.DMA(self.xk_in[l][128 * c:128 * c + 128, :], kb[slot][:, 0:T], [f"kb{slot}"], [f"xi{l}.k{c}"])
                self.CP("pool", self.ksn[:, c, :], kb[slot][:, T:TS], [f"kb{slot}"], [f"ksn{c}"])
            else:
                self.TTo("pool", kb[slot][:, 0:TS], t3[slot][:, 0:TS], t1[slot][:, 0:TS], ALU.add, [f"t3{slot}", f"t1{slot}"], [f"kb{slot}"])
                self.DMA(self.qT_s[128 * c:128 * c + 128, :], kb[slot][:, 0:TS], [f"kb{slot}"], [f"qT{c}"])

        def run_pipe(items, stages):
            n = len(items)
            for step in range(n + 2):
                if step < n:
                    qk_stage0(items[step])
                if 0 <= step - 1 < n:
                    qk_stage1(items[step - 1])
                if 0 <= step - 2 < n:
                    qk_stage2(items[step - 2])
                if step < n:
                    qk_stage0b(items[step])

        import os
        lvl = int(os.environ.get("IPLVL", "9"))
        nst = int(os.environ.get("IPNST", "3"))
        run_pipe([["k", c, c % 2] for c in range(8)], [qk_stage0, qk_stage1, qk_stage2][:nst])
        if lvl < 2:
            return
        banks = [0, 1, 2, 3, 4, 5]
        cnt = 0
        for st in range(4):
            W, wkey = self.get_w((l, f"v{st}"))
            for tj in range(int(os.environ.get("VNT", "9"))):
                r0, m = (128 * tj, 128) if tj < 8 else (T, NS + NH)
                b = banks[cnt % 6]
                sl = cnt % 2
                cnt += 1
                for kc in range(KC):
                    self.MM(self.ps[b][0:m, 0:256], self.xn[:, kc, r0:r0 + m], W[:, kc, :], kc == 0, kc == KC - 1,
                            [wkey, f"xn{kc}"], [f"ps{b}"])
                vm = os.environ.get("VMODE", "ab")
                if "a" in vm:
                    self.CP("act", vf[sl][0:m, :], self.ps[b][0:m, 0:256], [f"ps{b}"], [f"vf{sl}"])
                    self.DMA(self.v_out[l, r0:r0 + m, 256 * st:256 * st + 256], vf[sl][0:m, :], [f"vf{sl}"], [])
                if "b" in vm:
                    self.CP("dve", vb[sl][0:m, :], vf[sl][0:m, :], [f"vf{sl}"], [f"vb{sl}"])
                    if tj < 8:
                        self.DMA(self.xv_in[l][r0:r0 + 128, 256 * st:256 * st + 256], vb[sl], [f"vb{sl}"], [f"xi{l}.v{st}.{tj}"])
                    else:
                        self.CP("pool", self.vsn[:, 256 * st:256 * st + 256], vb[sl][0:NS, :], [f"vb{sl}"], [f"vsn{st}"])
        self.exchange(self.xk_in[l], self.xk_out[l], self.xi_keys(l)[:8], [f"xok{l}"], f"ccBk{l}")
        self.exchange(self.xv_in[l], self.xv_out[l], self.xi_keys(l)[8:], [f"xov{l}"], f"ccBv{l}")
        if lvl < 3:
            return
        run_pipe([["q", c, c % 2] for c in range(8)], [qk_stage0, qk_stage1, qk_stage2])
        if lvl < 4:
            return
        for st in range(4):
            W, wkey = self.get_w((l, f"z{st}"))
            for j in range(2):
                c = 2 * st + j
                sl = c % 2
                trip = self.next_trip()
                self.fm_tile(W, wkey, 128 * j, 128, xn_of, KC, trip, xk)
                for ti, (c0, n) in enumerate(CT):
                    self.CP("act" if ti == 0 else "dve", kb[sl][:, c0:c0 + n], self.ps[trip[ti]][:, 0:n], [f"ps{trip[ti]}"], [f"kb{sl}"])
                self.DMA(self.zT_s[128 * c:128 * c + 128, :], kb[sl][:, 0:TS], [f"kb{sl}"], [f"zT{c}"])
        if lvl < 5:
            return
        XL = 3 + T + 3 + NS + NH
        NO = XL - 3
        self.DMA(sst, self.sconv_s[l], [], ["sst"])
        for st in range(8):
            W, wkey = self.get_w((l, f"x{st}"))
            for j in range(2):
                c = 2 * st + j
                sl = c % 2
                trip = self.next_trip()
                self.fm_tile(W, wkey, 128 * j, 128, xn_of, KC, trip, xk)
                X, acc, xa = t1[sl], t2[sl], kb[sl]
                rk = [f"t1{sl}"]
                for ti, (c0, n) in enumerate(CT):
                    for (a_, b_) in self.segs(c0, n):
                        d0 = 3 + a_ if a_ < T else 1030 + (a_ - T)
                        self.CP("act" if ti < 2 else "dve", X[:, d0:d0 + (b_ - a_)], self.ps[trip[ti]][:, a_ - c0:b_ - c0], [f"ps{trip[ti]}"], rk)
                self.CP("pool", X[:, 1027:1030], sst[:, c, :], ["sst"] + rk, rk)
                self.TS("dve", X[:, 0:3], X[:, 1036:1039], self.FLAG, ALU.mult, rk + ["cf32"], rk)
                self.CP("pool", sco[:, c, 0:3], X[:, 1024:1027], rk, ["sco"])
                self.CP("pool", sco[:, c, 3:6], X[:, 1031:1034], rk, ["sco"])
                wcol = lambda t: self.pc(l, PC_SCW + 16 * t + c)
                ak = [f"t2{sl}"]
                self.A(acc[:, 0:NO], X[:, 3:3 + NO], AF.Identity, rk + ["pcol"], ak, bias=self.pc(l, PC_SCB + c), scale=wcol(3))
                for t in (2, 1, 0):
                    self.STT(acc[:, 0:NO], X[:, t:t + NO], wcol(t), acc[:, 0:NO], ALU.mult, ALU.add, rk + ak + ["pcol"], ak)
                self.A(xa[:, 0:NO], acc[:, 0:NO], AF.Silu, ak, [f"kb{sl}"])
                self.DMA(self.xbcT_s[128 * c:128 * c + 128, 0:T], xa[:, 0:T], [f"kb{sl}"], [f"xbcT{c}"])
                self.DMA(self.xbcT_s[128 * c:128 * c + 128, T:TS], xa[:, 1027:1031], [f"kb{sl}"], [f"xbcT{c}"])
        self.DMA(self.sconv_out[l], sco, ["sco"], [])
        if lvl < 6:
            return
        W, wkey = self.get_w((l, "dt"))
        trip = self.next_trip()
        self.fm_tile(W, wkey, 0, 16, xn_of, KC, trip, xk)
        dt, av = self.dta[:, 0, :], self.dta[:, 1, :]
        for ti, (c0, n) in enumerate(CT):
            n2 = min(n, TS - c0)
            self.A(dt[:, c0:c0 + n2], self.ps[trip[ti]][0:16, 0:n2], AF.Exp, [f"ps{trip[ti]}", "pcol"], ["dta"],
                   bias=self.pcol[0:16, PC_DTB:PC_DTB + 1], scale=1.0)
        self.A(dt, dt, AF.Ln, ["dta"], ["dta"], bias=1.0, scale=1.0)
        self.A(acol, self.pcol[0:16, PC_ALOG:PC_ALOG + 1], AF.Exp, ["pcol"], ["acol"])
        self.TS("dve", av, dt, acol, ALU.mult, ["dta", "acol"], ["dta"], s2=-1.0, op1=ALU.mult)

    def xi_keys(self, l):
        return [f"xi{l}.k{c}" for c in range(8)] + [f"xi{l}.v{st}.{tj}" for st in range(4) for tj in range(8)]

    def exchange(self, src, dst, rkeys, wkeys, stream):
        if self.cfg.no_cc:
            self.DMA(dst[0:src.shape[0], :], src, rkeys, wkeys)
            return
        rg = [[0, 1], [2, 3], [4, 5], [6, 7]]
        self.S.collective(lambda e: e.collective_compute("AllGather", ALU.bypass, replica_groups=rg, ins=[src], outs=[dst]),
                          reads=rkeys, writes=wkeys, stream=stream)

    def ssd_carve(self):
        o = 0
        c = {}
        def add(name, shape, dt):
            nonlocal o
            esz = 2 if dt == BF16 else 4
            nb = int(np.prod(shape[1:])) * esz
            nb = (nb + 31) // 32 * 32
            c[name] = self.carve(o, shape, dt)
            o += nb
        add("hT", [128, 16, 64], F32)
        add("hTs", [128, 16, 64], F32)
        add("acum", [16, TS], F32)
        self.ssd_persist = o
        add("xbc", [128, 16, 128], BF16)
        add("xstm", [128, 1024], BF16)
        add("btm", [128, 512], BF16)
        add("dtm", [128, 32], F32)
        add("acm", [128, 16], F32)
        add("XP", [128, 16, 128], BF16)
        add("XDD", [128, 16, 64], BF16)
        for q in range(2):
            add(f"X{q}", [128, 4, 128], F32)
            add(f"SEG{q}", [128, 4, 128], F32)
            add(f"EC{q}", [128, 4, 128], F32)
        add("WT", [128, 4, 128], BF16)
        add("CS", [128, 4, 128], BF16)
        add("HP", [128, 16, 128], BF16)
        assert o <= self.NSCR, o
        return c

    def ssd_local(self, l):
        c = self.ssd_carve()
        S = self.S
        hT, hTs, acum = c["hT"], c["hTs"], c["acum"]
        xbc, xstm, btm, dtm, acm = c["xbc"], c["xstm"], c["btm"], c["dtm"], c["acm"]
        XP, XDD, WT, CS, HP = c["XP"], c["XDD"], c["WT"], c["CS"], c["HP"]
        X2, SEG2, EC2 = [c["X0"], c["X1"]], [c["SEG0"], c["SEG1"]], [c["EC0"], c["EC1"]]
        S.op("pool", lambda e: e.memset(XP, 0.0), [], ["XP"])
        S.op("pool", lambda e: e.memset(HP, 0.0), [], ["HP"])
        psT = self.ps[0].bitcast(BF16)
        psT2 = self.ps[1].bitcast(BF16)
        IDB = self.cb["IDB"]
        for ch in range(9):
            c0, Lc = (128 * ch, 128) if ch < 8 else (T, NS)
            first = ch == 0 or ch == 8
            self.DMA(xbc[:, :, 0:Lc], self.xbcT_s.rearrange("(kc p) t -> p kc t", p=128)[:, :, c0:c0 + Lc],
                     [f"xbcT{k}" for k in range(16)], ["xbc"])
            for kc in range(8):
                self.TR(psT[0:Lc, 128 * kc:128 * kc + 128], xbc[:, kc, 0:Lc], IDB, ["xbc", "cbf"], ["ps0"])
            self.CP("act", xstm[0:Lc, :], psT[0:Lc, :], ["ps0"], ["xstm"])
            for kc in range(4):
                self.TR(psT2[0:Lc, 128 * kc:128 * kc + 128], xbc[:, 8 + kc, 0:Lc], IDB, ["xbc", "cbf"], ["ps1"])
            self.CP("act", btm[0:Lc, :], psT2[0:Lc, 0:512], ["ps1"], ["btm"])
            for k in range(2):
                self.TR(self.ps[2][0:Lc, 16 * k:16 * k + 16], self.dta[:, k, c0:c0 + Lc], self.IDF[0:16, 0:16], ["dta", "cf32"], ["ps2"])
            self.CP("dve", dtm[0:Lc, :], self.ps[2][0:Lc, 0:32], ["ps2"], ["dtm"])
            self.MM(self.ps[2][0:Lc, 32:48], self.UTRI[0:Lc, 0:Lc], dtm[0:Lc, 16:32], True, True, ["dtm", "cf32"], ["ps2"])
            self.CP("dve", acm[0:Lc, :], self.ps[2][0:Lc, 32:48], ["ps2"], ["acm"])
            self.MM(self.ps[2][0:16, 64:64 + Lc], dtm[0:Lc, 16:32], self.UTRI[0:Lc, 0:Lc], True, True, ["dtm", "cf32"], ["ps2"])
            if first:
                self.CP("dve", acum[:, c0:c0 + Lc], self.ps[2][0:16, 64:64 + Lc], ["ps2"], ["acum"])
            else:
                self.TS("dve", acum[:, c0:c0 + Lc], self.ps[2][0:16, 64:64 + Lc], acum[:, c0 - 1:c0], ALU.add, ["ps2", "acum"], ["acum"])
            def grp_bufs(g):
                q = g % 2
                return (X2[q], SEG2[q], EC2[q], f"X{q}", f"SEG{q}", f"EC{q}")

            def partA(g):
                X, SEG, EC, Xk, SEGk, ECk = grp_bufs(g)
                a4 = dtm[0:Lc, 16 + 4 * g:20 + 4 * g]
                d4 = dtm[0:Lc, 4 * g:4 * g + 4]
                Xv, SEGv, ECv = X[0:Lc, :, 0:Lc], SEG[0:Lc, :, 0:Lc], EC[:, :, 0:Lc]
                U3 = self.UTRI[0:Lc, 0:Lc].unsqueeze(1).to_broadcast([Lc, 4, Lc])
                self.TTo("dve", Xv, U3, a4.unsqueeze(2).to_broadcast([Lc, 4, Lc]), ALU.mult, ["dtm", "cf32"], [Xk])
                psA = self.ps[3][:, 0:4 * Lc].rearrange("p (h i) -> p h i", h=4)
                if Lc == 128:
                    self.MM(self.ps[3][:, 0:512], self.ONEF[0:Lc, :], X.rearrange("p h i -> p (h i)"), True, True, [Xk, "cf32"], ["ps3"])
                else:
                    for h in range(4):
                        self.MM(self.ps[3][:, Lc * h:Lc * h + Lc], self.ONEF[0:Lc, :], X[0:Lc, h, 0:Lc], True, True, [Xk, "cf32"], ["ps3"])
                self.TTo("dve", SEGv, psA[0:Lc], acm[0:Lc, 4 * g:4 * g + 4].unsqueeze(2).to_broadcast([Lc, 4, Lc]), ALU.subtract,
                         ["ps3", "acm"], [SEGk])
                self.A(SEGv, SEGv, AF.Exp, [SEGk], [SEGk])
                self.STT(SEGv, SEGv, 1.0, U3, ALU.min, ALU.mult, [SEGk, "cf32"], [SEGk])
                self.A(ECv, psA, AF.Exp, ["ps3"], [ECk])
            def partB(g):
                X, SEG, EC, Xk, SEGk, ECk = grp_bufs(g)
                a4 = dtm[0:Lc, 16 + 4 * g:20 + 4 * g]
                d4 = dtm[0:Lc, 4 * g:4 * g + 4]
                Xv, SEGv, ECv = X[0:Lc, :, 0:Lc], SEG[0:Lc, :, 0:Lc], EC[:, :, 0:Lc]
                psC = self.ps[2][0:Lc, 256:256 + Lc]
                self.MM(psC, xbc[:, 8 + g, 0:Lc], xbc[:, 12 + g, 0:Lc], True, True, ["xbc"], ["ps2"])
                self.TTo("dve", WT[0:Lc, :, 0:Lc], SEGv, psC.unsqueeze(1).to_broadcast([Lc, 4, Lc]), ALU.mult, [SEGk, "ps2"], ["WT"])
                if not first:
                    self.TTo("pool", CS[:, :, 0:Lc], ECv, xbc[:, 12 + g, 0:Lc].unsqueeze(1).to_broadcast([128, 4, Lc]), ALU.mult,
                             [ECk, "xbc"], ["CS"])
                xs4 = xstm[0:Lc, 256 * g:256 * g + 256].rearrange("p (u s d) -> p u s d", u=2, s=2)
                d44 = d4.rearrange("p (u s) -> p u s", u=2)
                for s_ in range(2):
                    self.TTo("dve", XP[0:Lc, 4 * g + s_:4 * g + 4:2, 64 * s_:64 * s_ + 64], xs4[:, :, s_, :],
                             d44[:, :, s_].unsqueeze(2).to_broadcast([Lc, 2, 64]), ALU.mult, ["xstm", "dtm"], ["XP"])
                xdd_o = XDD[0:Lc, 4 * g:4 * g + 4, :]
                dend = SEG[0:Lc, :, Lc - 1:Lc].to_broadcast([Lc, 4, 64])
                self.TTo("dve", xdd_o, xstm[0:Lc, 256 * g:256 * g + 256].rearrange("p (h d) -> p h d", h=4),
                         d4.unsqueeze(2).to_broadcast([Lc, 4, 64]), ALU.mult, ["xstm", "dtm"], ["XDD"])
                self.TTo("dve", xdd_o, xdd_o, dend, ALU.mult, ["XDD", SEGk], ["XDD"])
                for u in range(2):
                    pr = 2 * g + u
                    yb = 4 + pr // 4
                    yo = self.ps[yb][:, Lc * (pr % 4):Lc * (pr % 4) + Lc]
                    nmm = 2 if first else 4
                    k = 0
                    for s in range(2):
                        h = 2 * u + s
                        self.MM(yo, XP[0:Lc, 4 * g + h, :], WT[0:Lc, h, 0:Lc], k == 0, k == nmm - 1, ["XP", "WT"], [f"ps{yb}"])
                        k += 1
                    if not first:
                        for s in range(2):
                            h = 2 * u + s
                            self.MM(yo, HP[:, 4 * g + h, :], CS[:, h, 0:Lc], False, k == nmm - 1, ["HP", "CS"], [f"ps{yb}"])
                            k += 1
                sb_ = 6 + g // 2
                for h in range(4):
                    so = self.ps[sb_][:, 64 * (4 * (g % 2) + h):64 * (4 * (g % 2) + h) + 64]
                    self.MM(so, btm[0:Lc, 128 * g:128 * g + 128], XDD[0:Lc, 4 * g + h, :], True, True, ["btm", "XDD"], [f"ps{sb_}"])
                sv = self.ps[sb_][:, 256 * (g % 2):256 * (g % 2) + 256].rearrange("p (h d) -> p h d", h=4)
                if ch == 8:
                    self.CP("act", hTs[:, 4 * g:4 * g + 4, :], sv, [f"ps{sb_}"], ["hTs"])
                elif ch == 0:
                    self.CP("act", hT[:, 4 * g:4 * g + 4, :], sv, [f"ps{sb_}"], ["hT"])
                else:
                    hv = hT[:, 4 * g:4 * g + 4, :]
                    self.TTo("dve", hv, hv, EC[:, :, Lc - 1:Lc].to_broadcast([128, 4, 64]), ALU.mult, ["hT", ECk, "HP"], ["hT"])
                    self.TTo("dve", hv, hv, sv, ALU.add, ["hT", f"ps{sb_}"], ["hT"])
            for step in range(5):
                if step < 4:
                    partA(step)
                if step >= 1:
                    partB(step - 1)
            for pr in range(8):
                yb = 4 + pr // 4
                yo = self.ps[yb][:, Lc * (pr % 4):Lc * (pr % 4) + Lc]
                self.STT(self.xn[:, 8 + pr, c0:c0 + Lc], xbc[:, pr, 0:Lc], self.pc(l, PC_DCOL + pr), yo, ALU.mult, ALU.add,
                         ["xbc", f"ps{yb}", "pcol"], [f"xn{8 + pr}"])
            if ch < 7:
                for s_ in range(2):
                    self.CP("act", HP[:, s_:16:2, 64 * s_:64 * s_ + 64], hT[:, s_:16:2, :], ["hT"], ["HP"])
        self.DMA(self.xc_in[l], hT.rearrange("p h d -> p (h d)"), ["hT"], [f"xci{l}"])
        self.exchange(self.xc_in[l], self.xc_out[l], [f"xci{l}"], [f"xco{l}"], f"ccC{l}")

    def att_phase(self, l):
        base = self.ssd_persist
        o = base
        def cv(shape, dt):
            nonlocal o
            esz = 2 if dt == BF16 else 4
            nb = (int(np.prod(shape[1:])) * esz + 31) // 32 * 32
            a = self.carve(o, shape, dt)
            o += nb
            return a
        KT2 = [cv([128, 2048], BF16) for _ in range(2)]
        QT2 = [cv([128, TS], BF16) for _ in range(2)]
        V1 = cv([128, 9, 128], BF16)
        V4 = cv([128, 4, 3, 128], BF16)
        V16 = cv([128, 16, 128], BF16)
        KS = cv([128, 1152], BF16)
        VS = cv([128, 9, 128], BF16)
        NPE, NPP, DEPTH_PIPE = 3, 4, 3
        PE_ = [cv([128, 512], BF16) for _ in range(NPE)]
        PP = [cv([128, 512], BF16) for _ in range(NPP)]
        RD = cv([128, 512], F32)
        kin, kout, vin, vout = self.xk_in[l], self.xk_out[l], self.xv_in[l], self.xv_out[l]
        xi_keys = self.xi_keys(l)
        kk, vk = xi_keys[:8], xi_keys[8:]
        ONEB = self.cb["ONEB"]
        jobs = []

        def add_job(scores, post, pv, pre=None, fin=None, ld=None):
            jobs.append(dict(scores=scores, post=post, pv=pv, pre=pre, fin=fin, ld=ld))

        def loads_kq(c):
            KT, QT, q = KT2[c % 2], QT2[c % 2], c % 2
            self.DMA(KT[:, 0:1024], kout[128 * c:128 * c + 128, :], [f"xok{l}"], [f"KT{q}"])
            self.DMA(KT[:, 1024:2048], kin[128 * c:128 * c + 128, :], kk, [f"KT{q}"])
            self.DMA(QT, self.qT_s[128 * c:128 * c + 128, :], [f"qT{c}"], [f"QT{q}"])

        def loads_v(c):
            fs = slice(128 * c, 128 * c + 128)
            self.DMA(V1[:, 0, :], vout[896:1024, fs], [f"xov{l}"], ["V1"])
            self.DMA(V1[:, 1:9, :], vin[:, fs].rearrange("(b p) f -> p b f", p=128), vk, ["V1"])
            self.DMA(V4[:, :, 0, :], vout[512:1024, fs].rearrange("(i r) f -> i r f", r=4), [f"xov{l}"], ["V4"])
            for cb_ in range(2):
                self.DMA(V4[:, :, 1 + cb_, :], vin[512 * cb_:512 * cb_ + 512, fs].rearrange("(i r) f -> i r f", r=4), vk, ["V4"])
            self.DMA(V16[0:64], vout[0:1024, fs].rearrange("(m r) f -> m r f", r=16), [f"xov{l}"], ["V16"])
            self.DMA(V16[64:128], vin[:, fs].rearrange("(m r) f -> m r f", r=16), vk, ["V16"])
            self.DMA(KS, self.ksel[l, 128 * c:128 * c + 128, :], [], ["KS"], q="pool", stream="ks")
            self.DMA(VS, self.vsel[l, :, fs].rearrange("(s k) f -> k s f", k=128), [], ["VS"], q="pool", stream="vs")

        ji = [0]

        def make_batch(c, tt, s, mk, tl, first, last, is_first_of_pair):
            ktk, qtk = f"KT{c % 2}", f"QT{c % 2}"
            i = ji[0]
            ji[0] += 1
            sb_ = 4 + (i % 4)
            pe_, pek = PE_[i % NPE], f"PE{i % NPE}"
            P_, pk = PP[i % NPP], f"PP{i % NPP}"
            pr = slice(64 * s, 64 * s + 64)
            NUM, DEN = self.ps[s], self.ps[2 + s]
            nk, dk = f"ps{s}", f"ps{2 + s}"

            def scores():
                off = 0
                for (ka, qa, nq, vt, oc) in tl:
                    self.MM(self.ps[sb_][:, off:off + nq], ka, qa, True, True, [ktk, qtk], [f"ps{sb_}"])
                    off += nq

            def post():
                self.A(pe_, self.ps[sb_], AF.Exp, [f"ps{sb_}"], [pek], scale=0.125)
                self.TTo("dve", P_, pe_, self.cb[mk], ALU.mult, [pek, "cbf"], [pk])

            def pre():
                self.MM(NUM, self.cb["ZB"], self.cb["MUL"], True, False, ["cbf"], [nk])
                self.MM(DEN, self.cb["ZB"], self.cb["MUL"], True, False, ["cbf"], [dk])

            def pv():
                off = 0
                for (ka, qa, nq, vt, oc) in tl:
                    self.MM(oc(NUM)[pr], vt[:, pr], P_[:, off:off + nq], False, False, [pk, "V1", "V4", "V16"], [nk])
                    self.MM(oc(DEN)[pr], ONEB[:, 0:64], P_[:, off:off + nq], False, False, [pk, "cbf"], [dk])
                    off += nq

            def fin():
                self.S.op("dve", lambda e: e.reciprocal(out=RD[pr, :], in_=DEN[pr, :]), [dk], ["RD"])
                self.TTo("dve", self.xn[pr, c, 512 * tt:512 * tt + 512], NUM[pr, :], RD[pr, :], ALU.mult, [nk, "RD"], [f"xn{c}"])

            add_job(scores, post, pv, pre if first else None, fin if last else None, c if is_first_of_pair else None)

        def make_sample(c, s):
            i = ji[0]
            ji[0] += 1
            sb_ = 4 + (i % 4)
            pe_, pek = PE_[i % NPE], f"PE{i % NPE}"
            P_, pk = PP[i % NPP], f"PP{i % NPP}"
            pr = slice(64 * s, 64 * s + 64)
            NUM, DEN = self.ps[s], self.ps[2 + s]
            nk, dk = f"ps{s}", f"ps{2 + s}"
            qa = QT2[c % 2][pr, T:TS]
            qtk = f"QT{c % 2}"

            def scores():
                for st_ in range(9):
                    self.MM(self.ps[sb_][:, 4 * st_:4 * st_ + 4], KS[pr, 128 * st_:128 * st_ + 128], qa, True, True, ["KS", qtk], [f"ps{sb_}"])
                self.MM(self.ps[sb_][0:NS, 36:40], self.ksn[pr, c, :], qa, True, True, [f"ksn{c}", qtk], [f"ps{sb_}"])

            def post():
                self.A(pe_[:, 0:36], self.ps[sb_][:, 0:36], AF.Exp, [f"ps{sb_}"], [pek], scale=0.125)
                self.A(pe_[0:NS, 36:40], self.ps[sb_][0:NS, 36:40], AF.Exp, [f"ps{sb_}"], [pek], scale=0.125)
                self.TTo("dve", P_[:, 0:36], pe_[:, 0:36], self.cb["SM"][:, 0:36], ALU.mult, [pek, "cbf"], [pk])
                self.TTo("dve", P_[0:NS, 36:40], pe_[0:NS, 36:40], self.cb["SM"][0:NS, 36:40], ALU.mult, [pek, "cbf"], [pk])

            def pv():
                for st_ in range(9):
                    self.MM(NUM[pr, 0:NS], VS[:, st_, pr], P_[:, 4 * st_:4 * st_ + 4], st_ == 0, False, [pk, "VS"], [nk])
                    self.MM(DEN[pr, 0:NS], ONEB[:, 0:64], P_[:, 4 * st_:4 * st_ + 4], st_ == 0, False, [pk, "cbf"], [dk])
                self.MM(NUM[pr, 0:NS], self.vsn[:, 128 * c + 64 * s:128 * c + 64 * s + 64], P_[0:NS, 36:40], False, True, [pk] + [f"vsn{i_}" for i_ in range(4)], [nk])
                self.MM(DEN[pr, 0:NS], ONEB[0:NS, 0:64], P_[0:NS, 36:40], False, True, [pk, "cbf"], [dk])

            def fin():
                self.S.op("dve", lambda e: e.reciprocal(out=RD[pr, 0:NS], in_=DEN[pr, 0:NS]), [dk], ["RD"])
                self.TTo("dve", self.xn[pr, c, T:TS], NUM[pr, 0:NS], RD[pr, 0:NS], ALU.mult, [nk, "RD"], [f"xn{c}"])

            add_job(scores, post, pv, None, fin)
            jobs[-1]["last_of_pair"] = (s == 1)
            jobs[-1]["c"] = c

        for c in range(8):
            KT, QT = KT2[c % 2], QT2[c % 2]
            first_of_pair = True
            for tt in range(2):
                for s in range(2):
                    pr = slice(64 * s, 64 * s + 64)
                    batches = []
                    for half in range(2):
                        tl = []
                        for qb in (2 * half, 2 * half + 1):
                            cq = 8 + 4 * tt + qb
                            qa = QT[pr, 512 * tt + 128 * qb:512 * tt + 128 * qb + 128]
                            oc = lambda P, qb=qb: P[:, 128 * qb:128 * qb + 128]
                            tl.append((KT[pr, 128 * (cq - 1):128 * cq], qa, 128, V1[:, cq - 1 - 7, :], oc))
                            tl.append((KT[pr, 128 * cq:128 * cq + 128], qa, 128, V1[:, cq - 7, :], oc))
                        batches.append(("MULf1" if (tt == 0 and half == 0) else "MUL", tl))
                    for half in range(2):
                        tl = []
                        cq = 2 + tt
                        for r in (2 * half, 2 * half + 1):
                            qa = QT[pr, 512 * tt + r:512 * tt + 512:4]
                            oc = lambda P, r=r: P[:, r:512:4]
                            for cbk in (cq - 1, cq):
                                tl.append((KT[pr, 512 * cbk + r:512 * cbk + 512:4], qa, 128, V4[:, r, cbk - 1, :], oc))
                        batches.append(("MULf" if tt == 0 else "MUL", tl))
                    tl = []
                    for r in range(16):
                        qa = QT[pr, 512 * tt + r:512 * tt + 512:16]
                        oc = lambda P, r=r: P[:, r:512:16]
                        tl.append((KT[pr, r:2048:16], qa, 32, V16[:, r, :], oc))
                    batches.append((f"M16_{tt}", tl))
                    for bi, (mk, tl) in enumerate(batches):
                        make_batch(c, tt, s, mk, tl, bi == 0, bi == len(batches) - 1, first_of_pair)
                        first_of_pair = False
            for s in range(2):
                make_sample(c, s)
        pending = []

        def retire(job):
            if job["pre"]:
                job["pre"]()
            job["pv"]()
            if job["fin"]:
                job["fin"]()
            if job.get("last_of_pair") and job["c"] < 7:
                loads_v(job["c"] + 1)

        loads_kq(0)
        loads_v(0)
        for job in jobs:
            if job["ld"] is not None and job["ld"] < 7:
                loads_kq(job["ld"] + 1)
            job["scores"]()
            job["post"]()
            pending.append(job)
            if len(pending) > DEPTH_PIPE:
                retire(pending.pop(0))
        while pending:
            retire(pending.pop(0))

    def ssd_finish(self, l):
        c = self.ssd_carve()
        hT, hTs, acum = c["hT"], c["hTs"], c["acum"]
        o = self.ssd_persist
        def cv(shape, dt):
            nonlocal o
            esz = 2 if dt == BF16 else 4
            nb = (int(np.prod(shape[1:])) * esz + 31) // 32 * 32
            a = self.carve(o, shape, dt)
            o += nb
            return a
        H0 = [cv([128, 1024], F32) for _ in range(2)]
        H0b = [cv([128, 1024], BF16) for _ in range(2)]
        Gb = cv([16, TS], BF16)
        gl = cv([16, 2], F32)
        dg = cv([16, 32], F32)
        gtb = cv([128, 32], F32)
        CTa = cv([128, TS], BF16)
        gsb2 = [cv([128, 512], BF16) for _ in range(3)]
        tmp2 = [cv([128, 512], BF16) for _ in range(3)]
        zt2 = [cv([128, 512], BF16) for _ in range(4)]
        sq = [cv([128, 512], BF16) for _ in range(2)]
        rs = cv([128, 512], F32)
        self.DMA(H0[0], self.xc_out[l][0:128, :], [f"xco{l}"], ["H0p"])
        self.DMA(H0[1], self.sstate_s[l], [], ["H0s"])
        self.TS("dve", H0[0], H0[0], self.FLAG, ALU.mult, ["H0p", "cf32"], ["H0p"])
        self.CP("act", H0b[0], H0[0], ["H0p"], ["H0bp"])
        self.CP("act", H0b[1], H0[1], ["H0s"], ["H0bs"])
        self.A(Gb, acum, AF.Exp, ["acum"], ["Gb"])
        self.A(gl[:, 0:1], acum[:, T - 1:T], AF.Exp, ["acum"], ["gl"])
        self.A(gl[:, 1:2], acum[:, TS - 1:TS], AF.Exp, ["acum"], ["gl"])
        for k in range(2):
            self.TS("dve", dg[:, 16 * k:16 * k + 16], self.IDF[0:16, 0:16], gl[:, k:k + 1], ALU.mult, ["gl", "cf32"], ["dg"])
        self.MM(self.ps[2][:, 0:32], self.ONEF[0:16, :], dg, True, True, ["dg", "cf32"], ["ps2"])
        self.CP("act", gtb, self.ps[2][:, 0:32], ["ps2"], ["gtb"])
        for k, (hl, hk, h0k) in enumerate([(hT, "hT", "H0p"), (hTs, "hTs", "H0s")]):
            h0v = H0[k].rearrange("p (h d) -> p h d", h=16)
            self.TTo("dve", h0v, h0v, gtb[:, 16 * k:16 * k + 16].unsqueeze(2).to_broadcast([128, 16, 64]), ALU.mult, [h0k, "gtb", "H0bp", "H0bs"], [h0k])
            self.TTo("dve", h0v, h0v, hl, ALU.add, [h0k, hk], [h0k])
            self.DMA(self.sstate_out[l, k], H0[k], [h0k], [])
        EEXP = self.cb["EEXP"]
        tiles = [(0, 512, 0), (512, 512, 0), (T, NS, 1)]
        for pr in range(8):
            g = pr // 2
            if pr % 2 == 0:
                self.DMA(CTa, self.xbcT_s[1536 + 128 * g:1536 + 128 * g + 128, :], [f"xbcT{12 + g}"], ["CTa"])
            for (c0, n, k) in tiles:
                it = self._fin_i = getattr(self, "_fin_i", 0) + 1
                pa, pb = [(0, 1), (4, 5), (6, 7)][it % 3]
                gsb, tmp = gsb2[it % 3], tmp2[it % 3]
                gk, tk = f"gsb{it % 3}", f"tmp{it % 3}"
                self.MM(self.ps[pa][:, 0:n], H0b[k][:, 128 * pr:128 * pr + 128], CTa[:, c0:c0 + n], True, True, ["H0bp", "H0bs", "CTa"], [f"ps{pa}"])
                self.MM(self.ps[pb][:, 0:n], EEXP[0:16, 128 * pr:128 * pr + 128], Gb[:, c0:c0 + n], True, True, ["Gb", "cbf"], [f"ps{pb}"])
                self.CP("act", gsb[:, 0:n], self.ps[pb][:, 0:n], [f"ps{pb}"], [gk])
                self.TTo("dve", tmp[:, 0:n], self.ps[pa][:, 0:n], gsb[:, 0:n], ALU.mult, [f"ps{pa}", gk], [tk])
                yv = self.xn[:, 8 + pr, c0:c0 + n]
                self.TTo("pool", yv, yv, tmp[:, 0:n], ALU.add, [tk, f"xn{8 + pr}"], [f"xn{8 + pr}"])
        for (c0, n, k) in tiles:
            for pr in range(8):
                zt, zk = zt2[pr % 4], f"zt{pr % 4}"
                self.DMA(zt[:, 0:n], self.zT_s[128 * pr:128 * pr + 128, c0:c0 + n], [f"zT{pr}"], [zk])
                self.A(zt[:, 0:n], zt[:, 0:n], AF.Silu, [zk], [zk])
                yv = self.xn[:, 8 + pr, c0:c0 + n]
                self.TTo("dve", yv, yv, zt[:, 0:n], ALU.mult, [zk, f"xn{8 + pr}"], [f"xn{8 + pr}"])
                self.A(sq[pr % 2][:, 0:n], yv, AF.Square, [f"xn{8 + pr}"], [f"sq{pr % 2}"])
                self.MM(self.ps[3][:, 0:n], self.cb["ONEB"], sq[pr % 2][:, 0:n], pr == 0, pr == 7, [f"sq{pr % 2}", "cbf"], ["ps3"])
            self.A(rs[:, 0:n], self.ps[3][:, 0:n], AF.Sqrt, ["ps3", "cf32"], ["rs"], bias=self.EPSC, scale=1.0 / 1024)
            self.S.op("dve", lambda e, n=n: e.reciprocal(out=rs[:, 0:n], in_=rs[:, 0:n]), ["rs"], ["rs"])
            for pr in range(8):
                yv = self.xn[:, 8 + pr, c0:c0 + n]
                self.STT(yv, yv, self.pc(l, PC_SNG + pr), rs[:, 0:n], ALU.mult, ALU.mult, [f"xn{8 + pr}", "rs", "pcol"], [f"xn{8 + pr}"])

    def op_phase(self, l, last):
        mix_of = lambda kc: self.xn[:, kc, :]
        mk = lambda kc: f"xn{kc}"
        for st in range(8):
            W, wkey = self.get_w((l, f"o{st}"))
            for j in range(2):
                oc = 2 * st + j
                trip = self.next_trip()
                self.fm_tile(W, wkey, 128 * j, 128, mix_of, KC, trip, mk)
                for ti, (c0, n) in enumerate(CT):
                    xv = self.x[:, oc, c0:c0 + n]
                    self.TTo("dve", xv, xv, self.ps[trip[ti]][:, 0:n], ALU.add, [f"ps{trip[ti]}", f"x{oc}"], [f"x{oc}"])
        if True:
            xk = [f"x{kc}" for kc in range(KC)]
            self.DMA(self.xd_in[l].rearrange("kc (p j) -> p kc j", j=NH), self.x[:, :, T - NH:T], xk, [f"xdi{l}"])
            self.exchange(self.xd_in[l], self.xd_out[l], [f"xdi{l}"], [f"xdo{l}"], f"ccD{l}")
            self.DMA(self.x[:, :, TS:TT], self.xd_out[l][0:KC, :].rearrange("kc (p j) -> p kc j", j=NH), [f"xdo{l}"], xk)

    def ffn_phase(self, l):
        xn_of = lambda kc: self.xn[:, kc, :]
        xk = lambda kc: f"xn{kc}"
        HL = 2 + T + 2 + NS + NH
        NO = HL - 2
        o = 0
        def cv(shape, dt):
            nonlocal o
            esz = 2 if dt == BF16 else 4
            nb = (int(np.prod(shape[1:])) * esz + 31) // 32 * 32
            a_ = self.carve(o, shape, dt)
            o += nb
            return a_
        hp = [cv([128, 1040], F32) for _ in range(2)]
        acc = [cv([128, 1040], F32) for _ in range(2)]
        act = [cv([128, 1040], BF16) for _ in range(8)]
        fst = cv([128, 88, 2], F32)
        fco = cv([128, 88, 4], F32)
        self.DMA(fst, self.fconv_s[l], [], ["fst"])

        def half(f, which, W, wkey, j):
            fi = f + (44 if which else 0)
            trip = self.next_trip()
            self.fm_tile(W, wkey, 128 * j, 128, xn_of, KC, trip, xk)
            H = hp[which]
            hk = [f"hp{which}"]
            for ti, (c0, n) in enumerate(CT):
                for (a_, b_) in self.segs(c0, n):
                    d0 = 2 + a_ if a_ < T else 1028 + (a_ - T)
                    self.CP("act" if ti < 2 else "dve", H[:, d0:d0 + (b_ - a_)], self.ps[trip[ti]][:, a_ - c0:b_ - c0], [f"ps{trip[ti]}"], hk)
            self.CP("pool", H[:, 1026:1028], fst[:, fi, :], ["fst"] + hk, hk)
            self.TS("dve", H[:, 0:2], H[:, 1035:1037], self.FLAG, ALU.mult, hk + ["cf32"], hk)
            self.CP("pool", fco[:, fi, 0:2], H[:, 1024:1026], hk, ["fco"])
            self.CP("pool", fco[:, fi, 2:4], H[:, 1030:1032], hk, ["fco"])
            wcol = lambda t: self.pc(l, PC_FCW + 88 * t + fi)
            ak = [f"acc{which}"]
            self.A(acc[which][:, 0:NO], H[:, 2:2 + NO], AF.Identity, hk + ["pcol"], ak, bias=self.pc(l, PC_FCB + fi), scale=wcol(2))
            for t in (1, 0):
                self.STT(acc[which][:, 0:NO], H[:, t:t + NO], wcol(t), acc[which][:, 0:NO], ALU.mult, ALU.add, hk + ak + ["pcol"], ak)

        def unit(u):
            Wg, wgk = self.get_w((l, f"g{u}"))
            Wu, wuk = self.get_w((l, f"u{u}"))
            for j in range(2):
                f = 2 * u + j
                half(f, 0, Wg, wgk, j)
                half(f, 1, Wu, wuk, j)
                self.A(acc[0][:, 0:NO], acc[0][:, 0:NO], AF.Silu, ["acc0"], ["acc0"])
                self.TTo("pool", act[f % 8][:, 0:NO], acc[0][:, 0:NO], acc[1][:, 0:NO], ALU.mult, ["acc0", "acc1"], [f"act{f % 8}"])

        def down(g):
            Wa, wak = self.get_w((l, f"d{2 * g}"))
            Wb, wbk = self.get_w((l, f"d{2 * g + 1}"))
            cols = [(0, 345), (345, 345), (690, 345)]
            for oc in range(16):
                trip = self.next_trip()
                for kc in range(4):
                    W, wk = (Wa, wak) if kc < 2 else (Wb, wbk)
                    f = 4 * g + kc
                    for ti, (c0, n) in enumerate(cols):
                        self.MM(self.ps[trip[ti]][:, 0:n], W[:, kc % 2, 128 * oc:128 * oc + 128], act[f % 8][:, c0:c0 + n], kc == 0, kc == 3,
                                [wk, f"act{f % 8}"], [f"ps{trip[ti]}"])
                for ti, (c0, n) in enumerate(cols):
                    sg = []
                    if c0 < T:
                        sg.append((c0, min(c0 + n, T), c0))
                    if c0 + n > T + 2:
                        sg.append((max(c0, T + 2), c0 + n, max(c0, T + 2) - 2))
                    for (a_, b_, x0) in sg:
                        xv = self.x[:, oc, x0:x0 + (b_ - a_)]
                        self.TTo("dve", xv, xv, self.ps[trip[ti]][:, a_ - c0:b_ - c0], ALU.add, [f"ps{trip[ti]}", f"x{oc}"], [f"x{oc}"])

        for u in range(22):
            unit(u)
            if u >= 2 and u % 2 == 0:
                down(u // 2 - 1)
        down(10)
        self.DMA(self.fconv_out[l], fco, ["fco"], [])

    def build(self):
        cfg = self.cfg
        self.plan_weights()
        self.load_consts()
        self.S.tag = "scr"
        for l in range(self.L):
            if cfg.stop_after == "load":
                break
            last = l == self.L - 1
            self.S.tag = None
            self.DMA(self.pcol, self.pcol_d[:, l * NPC:(l + 1) * NPC], [], ["pcol"])
            self.barrier()
            self.norm_phase(l, PC_G1, 2048.0)
            if cfg.stop_after == "n1":
                self.S.tag = None
                self.DMA(self.mix_dbg, self.xn, [f"xn{kc}" for kc in range(KC)], [])
                break
            self.barrier()
            self.ip_phase(l)
            if cfg.stop_after == "ip":
                break
            self.barrier()
            self.ssd_local(l)
            self.barrier()
            self.att_phase(l)
            self.barrier()
            self.ssd_finish(l)
            if cfg.dbg and l == self.L - 1:
                self.DMA(self.mix_dbg, self.xn, [f"xn{kc}" for kc in range(KC)], [])
            if cfg.stop_after == "mix":
                break
            self.barrier()
            self.op_phase(l, last and cfg.stop_after is None)
            if cfg.stop_after == "op":
                break
            self.barrier()
            self.norm_phase(l, PC_G2, 2048.0)
            self.barrier()
            self.ffn_phase(l)
        self.S.tag = None
        yv = self.yT.rearrange("(kc p) t -> p kc t", p=128)
        for kc in range(KC):
            self.DMA(yv[:, kc, :], self.x[:, kc, :], [f"x{kc}"], [])
        self.S.emit()
        return self.nc


def _consts(hf):
    flag = np.float32(hf)
    lay, ncb = Builder.cbf_layout()
    cbf = np.zeros((128, ncb), np.float32)
    p = np.arange(128)[:, None]
    f = np.arange(128)[None, :]
    Uatt = (p >= f).astype(np.float32)
    Latt = (p <= f).astype(np.float32)
    def put(n, a):
        o, w = lay[n]
        cbf[:a.shape[0], o:o + w] = a
    put("MUL", np.concatenate([Uatt, Latt, Uatt, Latt], 1))
    put("MULf", np.concatenate([flag * Uatt, Latt, flag * Uatt, Latt], 1))
    put("MULf1", np.concatenate([flag * Uatt, Latt, Uatt, Latt], 1))
    for tt in range(2):
        m = np.zeros((128, 16, 32), np.float32)
        m[:64] = flag
        q = np.arange(32)[None, :]
        pp = np.arange(64)[:, None]
        m[64:] = (pp <= 32 * tt + q).astype(np.float32)[:, None, :]
        put(f"M16_{tt}", m.reshape(128, 512))
    put("IDB", np.eye(128, dtype=np.float32))
    put("ONEB", np.ones((128, 128), np.float32))
    bd = np.zeros((128, 128), np.float32)
    bd[:64, :64] = 1.0 / 64
    bd[64:, 64:] = 1.0 / 64
    put("BD64", bd)
    ee = np.zeros((16, 1024), np.float32)
    for h in range(16):
        ee[h, 64 * h:64 * h + 64] = 1.0
    put("EEXP", ee)
    sm = np.zeros((128, 40), np.float32)
    for l_ in range(4):
        sm[:, l_] = (np.arange(128) >= l_)
        for st in range(1, 9):
            if (st - 1) % 4 == l_:
                sm[:, 4 * st + l_] = 1.0
        for kj in range(4):
            sm[kj, 36 + l_] = 3.0 if kj == l_ else (1.0 if kj < l_ else 0.0)
    put("SM", sm)
    Rb = np.zeros((128, 128), np.float32)
    for m_ in range(128):
        if m_ % 64 < 32:
            Rb[m_ + 32, m_] = -1.0
        else:
            Rb[m_ - 32, m_] = 1.0
    put("RROTB", Rb)
    cf = np.zeros((128, 4 * 128 + 4), np.float32)
    cf[:, 0:128] = np.eye(128)
    cf[:, 128:256] = 1.0
    R = np.zeros((128, 128), np.float32)
    for m_ in range(128):
        if m_ % 64 < 32:
            R[m_ + 32, m_] = -1.0
        else:
            R[m_ - 32, m_] = 1.0
    cf[:, 256:384] = R
    cf[:, 384:512] = (p <= f)
    cf[:, 512] = flag
    cf[:, 513] = EPS
    half = 32
    inv = (np.float32(10000.0) ** (-np.arange(half, dtype=np.float32) / np.float32(half))).astype(np.float32)
    pos = np.concatenate([hf * T + np.arange(T), PAST + np.arange(NS), np.maximum(hf * T - NH + np.arange(NH), 0)]).astype(np.float32)
    ang = pos[None, :] * inv[:, None]
    cs = np.zeros((128, 2, TT), np.float32)
    cs[:, 0, :] = np.tile(np.cos(ang).astype(np.float32), (4, 1))
    cs[:, 1, :] = np.tile(np.sin(ang).astype(np.float32), (4, 1))
    return cbf.astype(bf16_np), cf, cs


def _pcol(inp):
    pc = np.zeros((128, DEPTH * NPC), np.float32)
    for l in range(DEPTH):
        b = l * NPC
        pc[:, b + PC_G1:b + PC_G1 + 16] = inp["norm1_g"][l].reshape(16, 128).T
        pc[:, b + PC_G2:b + PC_G2 + 16] = inp["norm2_g"][l].reshape(16, 128).T
        for t in range(4):
            pc[:, b + PC_SCW + 16 * t:b + PC_SCW + 16 * t + 16] = inp["ssd_conv_w"][l, t].reshape(16, 128).T
        pc[:, b + PC_SCB:b + PC_SCB + 16] = inp["ssd_conv_b"][l].reshape(16, 128).T
        for t in range(3):
            pc[:, b + PC_FCW + 88 * t:b + PC_FCW + 88 * t + 88] = inp["ffn_conv_w"][l, t].reshape(88, 128).T
        pc[:, b + PC_FCB:b + PC_FCB + 88] = inp["ffn_conv_b"][l].reshape(88, 128).T
        pc[:, b + PC_QG] = np.tile(inp["q_norm_g"][l], 2)
        pc[:, b + PC_KG] = np.tile(inp["k_norm_g"][l], 2)
        pc[:, b + PC_SNG:b + PC_SNG + 8] = inp["ssd_norm_g"][l].reshape(8, 128).T
        pc[:, b + PC_DCOL:b + PC_DCOL + 8] = np.repeat(inp["ssd_d"][l], 64).reshape(8, 128).T
        pc[0:16, b + PC_DTB] = inp["ssd_dt_bias"][l]
        pc[0:16, b + PC_ALOG] = inp["ssd_a_log"][l]
    return pc


_SEL_ROWS = np.concatenate([np.arange(1920, 2048)] + [1536 + l_ + 4 * np.arange(128) for l_ in range(4)]
                           + [l_ + 16 * np.arange(128) for l_ in range(4)])


def make_in_maps(inp, nl=DEPTH):
    f32 = np.float32
    xp, xs = inp["x_prompt"], inp["x_sample"]
    pcol = _pcol(inp)
    shared = {k: np.ascontiguousarray(inp[k][:nl], dtype=f32) for k in ("w_in", "w_out", "w_up", "w_down")}
    maps = []
    for c in range(8):
        b, hf = c // 2, c % 2
        cbf, cf, cs = _consts(hf)
        own = xp[b, hf * T:(hf + 1) * T]
        halo = xp[b, T - NH:T] if hf == 1 else np.zeros((NH, D), f32)
        xT0 = np.ascontiguousarray(np.concatenate([own, xs[c], halo], 0).T, dtype=f32)
        ck = inp["cache_win_k"][:, c][:, _SEL_ROWS]
        cv = inp["cache_win_v"][:, c][:, _SEL_ROWS]
        ksel = np.ascontiguousarray(ck.reshape(DEPTH, 1152, 1024).transpose(0, 2, 1), dtype=f32)
        vsel = np.ascontiguousarray(cv.reshape(DEPTH, 1152, 1024), dtype=f32)
        sconv_s = np.ascontiguousarray(inp["state_ssd_conv"][:, c].transpose(0, 2, 1).reshape(DEPTH, 16, 128, 3).transpose(0, 2, 1, 3), dtype=f32)
        sstate_s = np.ascontiguousarray(inp["state_ssd"][:, c].transpose(0, 3, 1, 2).reshape(DEPTH, 128, 1024), dtype=f32)
        fconv_s = np.ascontiguousarray(inp["state_ffn_conv"][:, c].transpose(0, 2, 1).reshape(DEPTH, 88, 128, 2).transpose(0, 2, 1, 3), dtype=f32)
        m = dict(shared)
        m.update(xT0=xT0, pcol=pcol, cossin=cs, cbf=cbf, cf32=cf, ksel=ksel[:nl], vsel=vsel[:nl], sconv_s=sconv_s[:nl],
                 sstate_s=sstate_s[:nl], fconv_s=fconv_s[:nl])
        maps.append(m)
    return maps


def assemble(results):
    f32 = np.float32
    L = DEPTH
    yp = np.zeros((4, 2048, D), f32)
    ys = np.zeros((8, NS, D), f32)
    wkp = np.zeros((L, 4, 2048, 16, 64), f32)
    wvp = np.zeros((L, 4, 2048, 16, 64), f32)
    wks = np.zeros((L, 8, NS, 16, 64), f32)
    wvs = np.zeros((L, 8, NS, 16, 64), f32)
    scp = np.zeros((L, 4, 3, 2048), f32)
    scs = np.zeros((L, 8, 3, 2048), f32)
    ssp = np.zeros((L, 4, 16, 64, 128), f32)
    sss = np.zeros((L, 8, 16, 64, 128), f32)
    fcp = np.zeros((L, 4, 2, 2 * D_FF), f32)
    fcs = np.zeros((L, 8, 2, 2 * D_FF), f32)
    for c in range(8):
        r = {k: np.asarray(v) for k, v in results[c].items()}
        b, hf = c // 2, c % 2
        yT = r["yT"]
        yp[b, hf * T:(hf + 1) * T] = yT[:, 0:T].T
        ys[c] = yT[:, T:TS].T
        kT = r["kT_out"]
        wkp[:, b, hf * T:(hf + 1) * T] = kT[:, :, 0:T].transpose(0, 2, 1).reshape(L, T, 16, 64)
        wks[:, c] = kT[:, :, T:TS].transpose(0, 2, 1).reshape(L, NS, 16, 64)
        v = r["v_out"]
        wvp[:, b, hf * T:(hf + 1) * T] = v[:, 0:T].reshape(L, T, 16, 64)
        wvs[:, c] = v[:, T:TS].reshape(L, NS, 16, 64)
        sc = r["sconv_out"]
        sc = sc.transpose(0, 2, 1, 3).reshape(L, 2048, 6)
        if hf == 1:
            scp[:, b] = sc[:, :, 0:3].transpose(0, 2, 1)
        scs[:, c] = sc[:, :, 3:6].transpose(0, 2, 1)
        st = r["sstate_out"]
        st = st.reshape(L, 2, 128, 16, 64).transpose(0, 1, 3, 4, 2)
        if hf == 1:
            ssp[:, b] = st[:, 0]
        sss[:, c] = st[:, 1]
        fc = r["fconv_out"]
        fc = fc.transpose(0, 2, 1, 3).reshape(L, 2 * D_FF, 4)
        if hf == 1:
            fcp[:, b] = fc[:, :, 0:2].transpose(0, 2, 1)
        fcs[:, c] = fc[:, :, 2:4].transpose(0, 2, 1)
    return (yp, ys, wkp, wvp, wks, wvs, scp, scs, ssp, sss, fcp, fcs)


_NC_CACHE = {}


def kernel(**inputs):
    inp = {k: np.asarray(v) for k, v in inputs.items()}
    if "nc" not in _NC_CACHE:
        _NC_CACHE["nc"] = Builder(Cfg()).build()
    nc = _NC_CACHE["nc"]
    maps = make_in_maps(inp)
    res = run_bass_kernel_spmd(nc, maps, core_ids=list(range(8)))
    return assemble(res.results)
```
